# Optimizing a Trainium2 kernel written in Bass

```python
import math
import jax
import jax.numpy as jnp
from jax import lax
import numpy as np

D_MODEL = 1024
BATCH = 4
SEQ = 4096
DEPTH = 1

D_PLE = 256
NSA_HEADS = 8
NSA_GROUPS = 2
NSA_HEAD_DIM = 64
CMP_BLOCK = 32
CMP_STRIDE = 16
CMP_HIDDEN = 256
SLC_BLOCK = 64
SLC_TOPK = 16
SLC_LOCAL = 2
WINDOW = 512
DIFF_HEADS = 4
DIFF_HEAD_DIM = 64
REL_BUCKETS = 32
REL_MAX_EXACT = 16
REL_MAX_DIST = 128
N_BIAS_HEADS = NSA_HEADS + DIFF_HEADS
D_FF = 2816
CONV_WIDTH = 3
Q_BLOCK = 128
ALPHA = (2.0 * DEPTH) ** 0.25
BETA = (8.0 * DEPTH) ** -0.25
LN_EPS = 1e-5
NEG_INF = -1e30
BIG = 1e30

NSA_Q_W = NSA_HEADS * NSA_HEAD_DIM
NSA_KV_W = NSA_GROUPS * NSA_HEAD_DIM
NSA_GATE_W = NSA_HEADS * 3
DIFF_QK_W = DIFF_HEADS * 2 * DIFF_HEAD_DIM
DIFF_V_W = DIFF_HEADS * 2 * DIFF_HEAD_DIM
IN_SIZES = (NSA_Q_W, NSA_KV_W, NSA_KV_W, NSA_KV_W, NSA_KV_W, NSA_KV_W, NSA_KV_W, NSA_GATE_W, DIFF_QK_W, DIFF_QK_W, DIFF_V_W, D_MODEL, D_MODEL)
IN_VALUE_PARTS = (2, 4, 6, 10)
D_IN = sum(IN_SIZES)

kernel_name = 'hybrid_nsa_diffattn_convffn_deepnorm'


def layer_norm(x, g, b):
    xf = x.astype(jnp.float32)
    mu = jnp.mean(xf, axis=-1, keepdims=True)
    xc = xf - mu
    var = jnp.mean(xc * xc, axis=-1, keepdims=True)
    return (xc * lax.rsqrt(var + LN_EPS) * g + b).astype(x.dtype)


def rms_norm(x, g):
    xf = x.astype(jnp.float32)
    return (xf * lax.rsqrt(jnp.mean(xf * xf, axis=-1, keepdims=True) + LN_EPS) * g).astype(x.dtype)


def rel_bucket(dist):
    n = jnp.maximum(dist, 0)
    large = REL_MAX_EXACT + (jnp.log(jnp.maximum(n, 1).astype(jnp.float32) / REL_MAX_EXACT)
                             / math.log(REL_MAX_DIST / REL_MAX_EXACT)
                             * (REL_BUCKETS - REL_MAX_EXACT)).astype(jnp.int32)
    large = jnp.minimum(large, REL_BUCKETS - 1)
    return jnp.where(n < REL_MAX_EXACT, n, large)


def compress_kv(kv, pe, w1, w2):
    B, S, G, Dh = kv.shape
    n_cmp = (S - CMP_BLOCK) // CMP_STRIDE + 1
    idx = np.arange(n_cmp)[:, None] * CMP_STRIDE + np.arange(CMP_BLOCK)[None, :]
    blocks = kv[:, idx] + pe[None, None, :, None, :]
    blocks = blocks.transpose(0, 1, 3, 2, 4).reshape(B, n_cmp, G, CMP_BLOCK * Dh)
    return jax.nn.gelu(blocks @ w1) @ w2


def slc_from_cmp(n_cmp, n_slc):
    ratio = SLC_BLOCK // CMP_STRIDE
    span = CMP_BLOCK // CMP_STRIDE
    j, m, n = np.meshgrid(np.arange(n_slc), np.arange(ratio), np.arange(span), indexing='ij')
    i = ratio * j + m - n
    ok = (i >= 0) & (i < n_cmp)
    mat = np.zeros((n_cmp, n_slc), np.float32)
    np.add.at(mat, (i[ok], j[ok]), 1.0)
    return jnp.asarray(mat)


def nsa_attention(q, k_cmp, v_cmp, k_slc, v_slc, k_win, v_win, gates,
                  pe_k, w1_k, w2_k, pe_v, w1_v, w2_v, table):
    B, S = q.shape[:2]
    G, R, Dh = NSA_GROUPS, NSA_HEADS // NSA_GROUPS, NSA_HEAD_DIM
    scale = Dh ** -0.5
    qg = q.reshape(B, S, G, R, Dh)
    pos = jnp.arange(S)
    tab = table[:, :NSA_HEADS].reshape(REL_BUCKETS, G, R)

    kc = compress_kv(k_cmp, pe_k, w1_k, w2_k)
    vc = compress_kv(v_cmp, pe_v, w1_v, w2_v)
    n_cmp = kc.shape[1]
    blk_end = jnp.arange(n_cmp) * CMP_STRIDE + CMP_BLOCK - 1
    dist_c = pos[:, None] - blk_end[None, :]
    valid_c = dist_c >= 0
    bias_c = jnp.transpose(tab[rel_bucket(dist_c)], (2, 3, 0, 1))
    logit_c = jnp.einsum('bsgrd,bngd->bgrsn', qg, kc).astype(jnp.float32) * scale + bias_c
    logit_c = jnp.where(valid_c, logit_c, NEG_INF)
    p_cmp = jnp.where(valid_c, jax.nn.softmax(logit_c, axis=-1), 0.0)
    o_cmp = jnp.einsum('bgrsn,bngd->bsgrd', p_cmp.astype(vc.dtype), vc)

    n_slc = S // SLC_BLOCK
    p_slc = jnp.einsum('bgrsn,nj->bgsj', p_cmp, slc_from_cmp(n_cmp, n_slc))
    j = jnp.arange(n_slc)[None, :]
    cur = (pos // SLC_BLOCK)[:, None]
    blk_valid = j <= cur
    forced = (j == 0) | ((cur - j >= 0) & (cur - j < SLC_LOCAL))
    score = jnp.where(forced, BIG, jnp.where(blk_valid, p_slc, NEG_INF))
    k_top = min(SLC_TOPK, n_slc)
    _, sel = lax.top_k(score, k_top)

    ks_blocks = k_slc.reshape(B, n_slc, SLC_BLOCK, G, Dh).transpose(0, 3, 1, 2, 4)
    vs_blocks = v_slc.reshape(B, n_slc, SLC_BLOCK, G, Dh).transpose(0, 3, 1, 2, 4)
    pad = ((0, 0), (WINDOW, 0), (0, 0), (0, 0))
    kw_pad = jnp.pad(k_win, pad)
    vw_pad = jnp.pad(v_win, pad)

    nqb = S // Q_BLOCK
    q_blocks = qg.reshape(B, nqb, Q_BLOCK, G, R, Dh).swapaxes(0, 1)
    sel_blocks = sel.reshape(B, G, nqb, Q_BLOCK, k_top).transpose(2, 0, 1, 3, 4)
    b_ix = jnp.arange(B)[:, None, None, None]
    g_ix = jnp.arange(G)[None, :, None, None]
    g_ix5 = jnp.arange(G)[None, :, None, None, None]
    blk_off = jnp.arange(SLC_BLOCK)
    win_off = jnp.arange(WINDOW + Q_BLOCK) - WINDOW

    def block_fn(args):
        qb, selb, ib = args
        q0 = ib * Q_BLOCK
        t = q0 + jnp.arange(Q_BLOCK)
        ks = ks_blocks[b_ix, g_ix, selb]
        vs = vs_blocks[b_ix, g_ix, selb]
        kpos = selb[..., None] * SLC_BLOCK + blk_off
        d_s = t[None, None, :, None, None] - kpos
        b_s = jnp.moveaxis(tab[rel_bucket(d_s), g_ix5], -1, 2)
        s = jnp.einsum('bqgrd,bgqkld->bgrqkl', qb, ks).astype(jnp.float32) * scale + b_s
        s = jnp.where((d_s >= 0)[:, :, None], s, NEG_INF)
        ps = jax.nn.softmax(s.reshape(B, G, R, Q_BLOCK, k_top * SLC_BLOCK), axis=-1)
        ps = ps.reshape(B, G, R, Q_BLOCK, k_top, SLC_BLOCK)
        o_s = jnp.einsum('bgrqkl,bgqkld->bqgrd', ps.astype(vs.dtype), vs)
        kw = lax.dynamic_slice_in_dim(kw_pad, q0, WINDOW + Q_BLOCK, axis=1)
        vw = lax.dynamic_slice_in_dim(vw_pad, q0, WINDOW + Q_BLOCK, axis=1)
        spos = q0 + win_off
        d_w = t[:, None] - spos[None, :]
        valid_w = (d_w >= 0) & (d_w < WINDOW) & (spos[None, :] >= 0)
        b_w = jnp.transpose(tab[rel_bucket(d_w)], (2, 3, 0, 1))
        sw = jnp.einsum('bqgrd,bsgd->bgrqs', qb, kw).astype(jnp.float32) * scale + b_w
        sw = jnp.where(valid_w, sw, NEG_INF)
        pw = jax.nn.softmax(sw, axis=-1)
        o_w = jnp.einsum('bgrqs,bsgd->bqgrd', pw.astype(vw.dtype), vw)
        return o_s, o_w

    o_slc, o_win = lax.map(block_fn, (q_blocks, sel_blocks, jnp.arange(nqb)))
    o_slc = o_slc.swapaxes(0, 1).reshape(B, S, G, R, Dh)
    o_win = o_win.swapaxes(0, 1).reshape(B, S, G, R, Dh)
    g = jax.nn.sigmoid(gates).reshape(B, S, G, R, 3)
    out = g[..., 0:1] * o_cmp + g[..., 1:2] * o_slc + g[..., 2:3] * o_win
    return out.reshape(B, S, NSA_Q_W)


def diff_attention(q, k, v, lq1, lk1, lq2, lk2, subln_g, table, lambda_init):
    B, S = q.shape[:2]
    Hd, d = DIFF_HEADS, DIFF_HEAD_DIM
    scale = d ** -0.5
    q = q.reshape(B, S, Hd, 2, d)
    k = k.reshape(B, S, Hd, 2, d)
    v = v.reshape(B, S, Hd, 2 * d)
    lam = (jnp.exp(jnp.sum(lq1 * lk1).astype(jnp.float32))
           - jnp.exp(jnp.sum(lq2 * lk2).astype(jnp.float32)) + lambda_init)
    tab = table[:, NSA_HEADS:]
    kpos = jnp.arange(S)
    nqb = S // Q_BLOCK
    q_blocks = q.reshape(B, nqb, Q_BLOCK, Hd, 2, d).swapaxes(0, 1)

    def block_fn(args):
        qb, ib = args
        t = ib * Q_BLOCK + jnp.arange(Q_BLOCK)
        dist = t[:, None] - kpos[None, :]
        bias = jnp.transpose(tab[rel_bucket(dist)], (2, 0, 1))[:, None]
        s = jnp.einsum('bqhcd,bshcd->bhcqs', qb, k).astype(jnp.float32) * scale + bias
        s = jnp.where(dist >= 0, s, NEG_INF)
        a = jax.nn.softmax(s, axis=-1)
        attn = a[:, :, 0] - lam * a[:, :, 1]
        return jnp.einsum('bhqs,bshe->bqhe', attn.astype(v.dtype), v)

    o = lax.map(block_fn, (q_blocks, jnp.arange(nqb)))
    o = o.swapaxes(0, 1).reshape(B, S, Hd, 2 * d)
    o = rms_norm(o, subln_g) * (1.0 - lambda_init)
    return o.reshape(B, S, DIFF_V_W)


def causal_dwconv(h, w, b):
    S = h.shape[1]
    hp = jnp.pad(h, ((0, 0), (CONV_WIDTH - 1, 0), (0, 0)))
    out = b
    for kk in range(CONV_WIDTH):
        out = out + w[kk] * hp[:, kk:kk + S]
    return out


def setup_inputs(seed: int = 0) -> dict:
    key = jax.random.key(seed)
    ks = jax.random.split(key, 32)

    def nrm(k, shape, scale):
        return jax.random.normal(k, shape, jnp.float32) * scale

    col_scale = np.concatenate([np.full((s,), BETA if i in IN_VALUE_PARTS else 1.0, np.float32)
                                for i, s in enumerate(IN_SIZES)])
    cmp_in = CMP_BLOCK * NSA_HEAD_DIM
    return {
        'x': nrm(ks[0], (BATCH, SEQ, D_MODEL), 1.0),
        'p': nrm(ks[1], (DEPTH, BATCH, SEQ, D_PLE), 1.0),
        'w_in': nrm(ks[2], (DEPTH, D_MODEL, D_IN), D_MODEL ** -0.5) * jnp.asarray(col_scale),
        'nsa_cmp_pe_k': nrm(ks[3], (DEPTH, CMP_BLOCK, NSA_HEAD_DIM), 0.1),
        'nsa_cmp_w1_k': nrm(ks[4], (DEPTH, cmp_in, CMP_HIDDEN), cmp_in ** -0.5),
        'nsa_cmp_w2_k': nrm(ks[5], (DEPTH, CMP_HIDDEN, NSA_HEAD_DIM), CMP_HIDDEN ** -0.5),
        'nsa_cmp_pe_v': nrm(ks[6], (DEPTH, CMP_BLOCK, NSA_HEAD_DIM), 0.1),
        'nsa_cmp_w1_v': nrm(ks[7], (DEPTH, cmp_in, CMP_HIDDEN), cmp_in ** -0.5),
        'nsa_cmp_w2_v': nrm(ks[8], (DEPTH, CMP_HIDDEN, NSA_HEAD_DIM), CMP_HIDDEN ** -0.5),
        'diff_lambda_q1': nrm(ks[9], (DEPTH, DIFF_HEAD_DIM), 0.1),
        'diff_lambda_k1': nrm(ks[10], (DEPTH, DIFF_HEAD_DIM), 0.1),
        'diff_lambda_q2': nrm(ks[11], (DEPTH, DIFF_HEAD_DIM), 0.1),
        'diff_lambda_k2': nrm(ks[12], (DEPTH, DIFF_HEAD_DIM), 0.1),
        'diff_subln_g': 1.0 + nrm(ks[13], (DEPTH, 2 * DIFF_HEAD_DIM), 0.02),
        'w_branch_nsa': nrm(ks[14], (DEPTH, NSA_Q_W, D_MODEL), NSA_Q_W ** -0.5 * BETA),
        'w_branch_diff': nrm(ks[15], (DEPTH, DIFF_V_W, D_MODEL), DIFF_V_W ** -0.5 * BETA),
        'w_out': nrm(ks[16], (DEPTH, D_MODEL, D_MODEL), D_MODEL ** -0.5 * BETA),
        'ln1_g': 1.0 + nrm(ks[17], (DEPTH, D_MODEL), 0.02),
        'ln1_b': nrm(ks[18], (DEPTH, D_MODEL), 0.02),
        'w_ffn_in': nrm(ks[19], (DEPTH, D_MODEL, 2 * D_FF), D_MODEL ** -0.5),
        'ffn_conv_w': nrm(ks[20], (DEPTH, CONV_WIDTH, D_FF), CONV_WIDTH ** -0.5),
        'ffn_conv_b': nrm(ks[21], (DEPTH, D_FF), 0.02),
        'w_ffn_down': nrm(ks[22], (DEPTH, D_FF, D_MODEL), D_FF ** -0.5 * BETA),
        'ln2_g': 1.0 + nrm(ks[23], (DEPTH, D_MODEL), 0.02),
        'ln2_b': nrm(ks[24], (DEPTH, D_MODEL), 0.02),
        'w_ple_proj': nrm(ks[25], (DEPTH, D_PLE, D_MODEL), D_PLE ** -0.5),
        'w_ple_gate': nrm(ks[26], (DEPTH, D_MODEL, D_MODEL), D_MODEL ** -0.5),
        'rel_bias_table': nrm(ks[27], (REL_BUCKETS, N_BIAS_HEADS), 0.3),
    }


def reference(x, p, w_in, nsa_cmp_pe_k, nsa_cmp_w1_k, nsa_cmp_w2_k, nsa_cmp_pe_v, nsa_cmp_w1_v,
              nsa_cmp_w2_v, diff_lambda_q1, diff_lambda_k1, diff_lambda_q2, diff_lambda_k2,
              diff_subln_g, w_branch_nsa, w_branch_diff, w_out, ln1_g, ln1_b, w_ffn_in,
              ffn_conv_w, ffn_conv_b, w_ffn_down, ln2_g, ln2_b, w_ple_proj, w_ple_gate,
              rel_bias_table):
    B, S, _ = x.shape
    splits = [int(c) for c in np.cumsum(IN_SIZES)[:-1]]
    kv_shape = (B, S, NSA_GROUPS, NSA_HEAD_DIM)
    for l in range(DEPTH):
        lambda_init = 0.8 - 0.6 * math.exp(-0.3 * l)
        proj = x @ w_in[l]
        (nsa_q, k_cmp, v_cmp, k_slc, v_slc, k_win, v_win, nsa_g,
         d_q, d_k, d_v, gate_nsa, gate_diff) = jnp.split(proj, splits, axis=-1)
        y_nsa = nsa_attention(nsa_q, k_cmp.reshape(kv_shape), v_cmp.reshape(kv_shape),
                              k_slc.reshape(kv_shape), v_slc.reshape(kv_shape),
                              k_win.reshape(kv_shape), v_win.reshape(kv_shape), nsa_g,
                              nsa_cmp_pe_k[l], nsa_cmp_w1_k[l], nsa_cmp_w2_k[l],
                              nsa_cmp_pe_v[l], nsa_cmp_w1_v[l], nsa_cmp_w2_v[l], rel_bias_table)
        y_diff = diff_attention(d_q, d_k, d_v, diff_lambda_q1[l], diff_lambda_k1[l],
                                diff_lambda_q2[l], diff_lambda_k2[l], diff_subln_g[l],
                                rel_bias_table, lambda_init)
        merged = (jax.nn.sigmoid(gate_nsa) * (y_nsa @ w_branch_nsa[l])
                  + jax.nn.sigmoid(gate_diff) * (y_diff @ w_branch_diff[l]))
        x = layer_norm(ALPHA * x + merged @ w_out[l], ln1_g[l], ln1_b[l])
        gu = x @ w_ffn_in[l]
        g, u = jnp.split(gu, 2, axis=-1)
        g = causal_dwconv(g, ffn_conv_w[l], ffn_conv_b[l])
        x = layer_norm(ALPHA * x + (jax.nn.gelu(g) * u) @ w_ffn_down[l], ln2_g[l], ln2_b[l])
        x = x + jax.nn.sigmoid(x @ w_ple_gate[l]) * (p[l] @ w_ple_proj[l])
    return x
```

```python
import math
import contextlib
import numpy as np
import concourse.bass as bass
import concourse.mybir as mybir
from concourse.bass_utils import run_bass_kernel_spmd

F32 = mybir.dt.float32
BF16 = mybir.dt.bfloat16
AF = mybir.ActivationFunctionType
ALU = mybir.AluOpType
AX = mybir.AxisListType

S = 4096
D = 1024
NT = S // 128
NQB = S // 512
QB0 = 3
OWN0 = 2048
NOWN = 2048
DFF = 2816
NFC = DFF // 128
LFV = 4608
OFFD = 2064
NEG = -30000.0
ALPHA = 2.0 ** 0.25
EPS = 1e-5
DEBUG = None
DEBUG_OUT = ()


class Prog:
    ENG = ('pe', 'act', 'dve', 'pool', 'sp')
    KDMA = 14
    CAP = 1000

    def __init__(self, nc, es):
        self.nc = nc
        self.es = es
        self.nsem = 0
        self.q = {e: [] for e in self.ENG}
        self.sem = {e: self._newsem() for e in ('pe', 'act', 'dve', 'pool')}
        self.cnt = {e: 0 for e in ('pe', 'act', 'dve', 'pool')}
        self.last = {}
        self.dslot = {e: [[self._newsem(), 0] for _ in range(self.KDMA)] for e in ('sp', 'pool', 'act')}
        self.dlast = {e: [None] * self.KDMA for e in ('sp', 'pool', 'act')}
        self.dcnt = {e: 0 for e in ('sp', 'pool', 'act')}
        self.waited = {}
        self.res = {}
        self.nops = 0

    def _newsem(self):
        self.nsem += 1
        return self.es.enter_context(self.nc.semaphore('sm%d' % self.nsem))

    def _deps(self, reads, writes):
        deps = {}

        def add(tok):
            if tok is None:
                return
            s, v, e = tok
            if id(s) not in deps or deps[id(s)][1] < v:
                deps[id(s)] = (s, v, e)
        for r in reads:
            st = self.res.get(r)
            if st:
                add(st['w'])
        for r in writes:
            st = self.res.get(r)
            if st:
                add(st['w'])
                for t in st['r'].values():
                    add(t)
        return deps

    def _commit(self, tok, reads, writes):
        for r in reads:
            st = self.res.setdefault(r, {'w': None, 'r': {}})
            st['r'][id(tok[0])] = tok
        for r in writes:
            self.res[r] = {'w': tok, 'r': {}}

    def _waits(self, eng, deps, skip_eng=None):
        waits = []
        for _, (s, v, e) in deps.items():
            if skip_eng is not None and e == skip_eng:
                continue
            key = (eng, id(s))
            if self.waited.get(key, 0) >= v:
                continue
            self.waited[key] = v
            waits.append((s, v))
        return waits

    def op(self, eng, fn, reads=(), writes=()):
        self.nops += 1
        deps = self._deps(reads, writes)
        waits = self._waits(eng, deps, skip_eng=('pe' if eng == 'pe' else None))
        if self.cnt[eng] >= self.CAP:
            self.sem[eng] = self._newsem()
            self.cnt[eng] = 0
        self.cnt[eng] += 1
        tok = (self.sem[eng], self.cnt[eng], eng)
        self.last[eng] = tok
        self.q[eng].append((waits, fn, tok[0], 1))
        self._commit(tok, reads, writes)
        return tok

    def dma(self, eng, fn, reads=(), writes=()):
        self.nops += 1
        deps = self._deps(reads, writes)
        n = self.dcnt[eng]
        self.dcnt[eng] += 1
        k = n % self.KDMA
        prev = self.dlast[eng][k]
        if prev is not None and id(prev[0]) not in deps:
            deps[id(prev[0])] = prev
        slot = self.dslot[eng][k]
        if slot[1] + 16 > self.CAP:
            slot[0] = self._newsem()
            slot[1] = 0
        slot[1] += 16
        tok = (slot[0], slot[1], 'dma_' + eng)
        self.dlast[eng][k] = tok
        waits = self._waits(eng, deps)
        self.q[eng].append((waits, fn, tok[0], 16))
        self._commit(tok, reads, writes)
        return tok

    def barrier(self):
        toks = [t for t in self.last.values()]
        for e in ('sp', 'pool', 'act'):
            toks += [t for t in self.dlast[e] if t is not None]
        for eng in self.ENG:
            deps = {id(t[0]): t for t in toks}
            waits = self._waits(eng, deps, skip_eng=(eng if eng in self.sem else None))
            if waits:
                self.q[eng].append((waits, None, None, 0))

    def emit(self):
        nc = self.nc
        q = self.q
        with nc.Block() as block:
            def runner(name):
                def f(e):
                    for waits, fn, s, inc in q[name]:
                        for ws, wv in waits:
                            e.wait_ge(ws, wv)
                        if fn is not None:
                            fn(e).then_inc(s, inc)
                return f
            block.tensor(runner('pe'))
            block.scalar(runner('act'))
            block.vector(runner('dve'))
            block.gpsimd(runner('pool'))
            block.sync(runner('sp'))
        self.q = {e: [] for e in self.ENG}


class Rot:
    def __init__(self, name, tiles):
        self.name = name
        self.tiles = tiles
        self.i = 0

    def next(self):
        k = self.i % len(self.tiles)
        self.i += 1
        return self.tiles[k], '%s%d' % (self.name, k)


def rel_bucket_np(n):
    n = np.maximum(n, 0)
    large = 16 + (np.log(np.maximum(n, 1).astype(np.float32) / np.float32(16)) / np.float32(math.log(128 / 16))
                  * np.float32(16)).astype(np.int32)
    large = np.minimum(large, 31)
    return np.where(n < 16, n, large)


def cmp_tile_type(q0, nq, nt):
    lo = q0 - (16 * (128 * nt + 127) + 31)
    hi = q0 + nq - 1 - (16 * (128 * nt) + 31)
    if hi < 0:
        return 'skip'
    if lo >= 113 and nt == 0:
        return 'zero'
    return 'bias'


def host_consts(hf=1):
    c = {}
    voff = 0 if hf == 1 else 2048
    a = np.arange(LFV)
    dist = a - OFFD
    oh = np.zeros((33, LFV), np.float32)
    b = rel_bucket_np(dist)
    ok = dist >= 0
    oh[b[ok], a[ok]] = 1.0
    oh[31, a[ok]] -= 1.0
    oh[32, a[~ok]] = 1.0
    c['ohd'] = oh
    c['ident'] = np.eye(128, dtype=np.float32)
    ut = np.zeros((128, 6, 4, 128), np.float32)
    for m in range(5):
        for i in range(4):
            if m - 1 - i > 0:
                ut[:, m, i, :] = NEG
    k = np.arange(128)[:, None]
    q = np.arange(128)[None, :]
    w4 = np.where(k > q, 0.0, NEG).astype(np.float32)
    ut[:, 5, 3, :] = w4
    c['utmpl'] = ut.reshape(128, 6 * 512)
    wc = np.zeros((128, 3, 4, 128), np.float32)
    for jj in range(3):
        for i in range(4):
            r = jj - 4 - i
            if r == -4:
                wc[:, jj, i, :] = w4
            elif r < -4:
                wc[:, jj, i, :] = NEG
    c['wctmpl'] = wc.reshape(128, 3 * 512)
    e = np.zeros((64, S), np.float32)
    e[np.arange(S) // 64, np.arange(S)] = 1.0
    c['eall'] = e
    n_cmp, n_slc = 255, 64
    jj, mm, nn = np.meshgrid(np.arange(n_slc), np.arange(4), np.arange(2), indexing='ij')
    ii = 4 * jj + mm - nn
    okk = (ii >= 0) & (ii < n_cmp)
    mat = np.zeros((256, n_slc), np.float32)
    np.add.at(mat, (ii[okk], jj[okk]), 1.0)
    cv = np.ones((256, 1), np.float32)
    cv[255] = 0.0
    cv[:voff // 16] = 0.0
    mat = mat * cv
    c['mslc'] = mat
    c['cvalid'] = np.ascontiguousarray(cv.reshape(2, 128).T)
    tv = np.arange(S)
    cur = (tv - voff) // 64
    j = np.arange(64)[None, :] - voff // 64
    forced = (j == 0) | ((cur[:, None] - j >= 0) & (cur[:, None] - j < 2))
    valid = (j <= cur[:, None]) & (j >= 0)
    am = np.where(valid & forced, 1e4, np.where(valid, 0.0, -1e4)).astype(np.float32)
    am[tv < voff] = 0.0
    vc_ = (np.arange(NT) * 128 >= voff).astype(np.float32)
    c['validc'] = np.ascontiguousarray(np.broadcast_to(vc_[None, :], (128, NT)))
    c['hv'] = np.full((128, 1), float(hf), np.float32)
    c['addmask'] = np.ascontiguousarray(am.reshape(NT, 128, 64).transpose(1, 0, 2)).reshape(128, NT * 64)
    return c


def prep_inputs(inputs, b, hf, consts):
    f = lambda a: np.ascontiguousarray(np.asarray(a, dtype=np.float32))
    w_in = f(inputs['w_in'])[0]
    offs = np.cumsum([0, 512, 128, 128, 128, 128, 128, 128, 24, 512, 512, 512, 1024, 1024])
    seg = lambda i: w_in[:, offs[i]:offs[i + 1]]
    nq = seg(0).reshape(D, 2, 4, 64)
    qn = [np.concatenate([nq[:, 0, r, :], nq[:, 1, r, :]], axis=1) for r in range(4)]
    w_fm = np.concatenate([seg(3), seg(5), seg(1), seg(2), seg(9)] + qn + [seg(8), seg(11), seg(12)], axis=1)
    w_tm = np.concatenate([seg(4), seg(6), seg(10), seg(7)], axis=1)
    m = dict(consts)
    tab = f(inputs['rel_bias_table'])
    m['tabaug'] = np.concatenate([tab, np.full((1, 12), NEG, np.float32)], axis=0)
    xb_ = f(inputs['x'])[b]
    m['x'] = xb_ if hf == 1 else np.concatenate([np.zeros((2048, D), np.float32), xb_[:2048]], axis=0)
    m['p'] = np.ascontiguousarray(f(inputs['p'])[0, b, hf * 2048:(hf + 1) * 2048])
    m['w_fm'] = np.ascontiguousarray(w_fm)
    m['w_tm'] = np.ascontiguousarray(w_tm)
    m['w1k'] = f(inputs['nsa_cmp_w1_k'])[0]; m['w1v'] = f(inputs['nsa_cmp_w1_v'])[0]
    m['w2k'] = f(inputs['nsa_cmp_w2_k'])[0]; m['w2v'] = f(inputs['nsa_cmp_w2_v'])[0]
    m['pekT'] = np.ascontiguousarray(f(inputs['nsa_cmp_pe_k'])[0].T)
    m['pevT'] = np.ascontiguousarray(f(inputs['nsa_cmp_pe_v'])[0].T)
    m['lam_in'] = np.concatenate([f(inputs['diff_lambda_q1']), f(inputs['diff_lambda_k1']),
                                  f(inputs['diff_lambda_q2']), f(inputs['diff_lambda_k2'])], axis=0).reshape(1, 256)
    m['subln'] = f(inputs['diff_subln_g']).reshape(1, 128)
    m['w_bn'] = f(inputs['w_branch_nsa'])[0]; m['w_bd'] = f(inputs['w_branch_diff'])[0]
    m['w_out'] = f(inputs['w_out'])[0]
    m['ln1g'] = f(inputs['ln1_g']).reshape(1, D); m['ln1b'] = f(inputs['ln1_b']).reshape(1, D)
    m['ln2g'] = f(inputs['ln2_g']).reshape(1, D); m['ln2b'] = f(inputs['ln2_b']).reshape(1, D)
    m['w_ffn'] = f(inputs['w_ffn_in'])[0]
    cw = f(inputs['ffn_conv_w'])[0]
    m['convw'] = np.ascontiguousarray(cw.reshape(3, NFC, 128).transpose(2, 1, 0)).reshape(128, NFC * 3)
    m['convb'] = np.ascontiguousarray(f(inputs['ffn_conv_b'])[0].reshape(NFC, 128).T)
    m['w_down'] = f(inputs['w_ffn_down'])[0]
    m['w_pp'] = f(inputs['w_ple_proj'])[0]; m['w_pg'] = f(inputs['w_ple_gate'])[0]
    return m


def kernel(**inputs):
    nc, dbg = build_nc()
    consts = [host_consts(0), host_consts(1)]
    in_maps = [prep_inputs(inputs, c // 2, c % 2, consts[c % 2]) for c in range(8)]
    res = run_bass_kernel_spmd(nc, in_maps, core_ids=list(range(8)))
    if dbg is not None:
        return res
    full = np.empty((4, S, D), np.float32)
    for c in range(8):
        full[c // 2, (c % 2) * 2048:(c % 2 + 1) * 2048] = np.asarray(res.results[c]['out'], dtype=np.float32)
    return full


def build_nc():
    nc = bass.Bass("TRN2", target_bir_lowering=False)

    def din(name, shape, dt=F32):
        return nc.dram_tensor(name, list(shape), dt, kind="ExternalInput").ap()

    def dscr(name, shape, dt=BF16):
        kind = "ExternalOutput" if (DEBUG and name in DEBUG_OUT) else "Internal"
        return nc.dram_tensor(name, list(shape), dt, kind=kind).ap()

    x = din('x', [S, D])
    pin = din('p', [NOWN, 256])
    validc = din('validc', [128, NT])
    hvin = din('hv', [128, 1])
    w_fm = din('w_fm', [D, 4096])
    w_tm = din('w_tm', [D, 792])
    ohd = din('ohd', [33, LFV])
    tabaug = din('tabaug', [33, 12])
    ident_d = din('ident', [128, 128])
    utmpl = din('utmpl', [128, 6 * 512])
    wctmpl = din('wctmpl', [128, 3 * 512])
    eall = din('eall', [64, S])
    mslc = din('mslc', [256, 64])
    cvalid = din('cvalid', [128, 2])
    addmask = din('addmask', [128, NT * 64])
    w1k = din('w1k', [2048, 256]); w1v = din('w1v', [2048, 256])
    w2k = din('w2k', [256, 64]); w2v = din('w2v', [256, 64])
    pekT = din('pekT', [64, 32]); pevT = din('pevT', [64, 32])
    lam_in = din('lam_in', [1, 256])
    subln = din('subln', [1, 128])
    w_bn = din('w_bn', [512, D]); w_bd = din('w_bd', [512, D]); w_out = din('w_out', [D, D])
    ln1g = din('ln1g', [1, D]); ln1b = din('ln1b', [1, D]); ln2g = din('ln2g', [1, D]); ln2b = din('ln2b', [1, D])
    w_ffn = din('w_ffn', [D, 2 * DFF])
    convw = din('convw', [128, NFC * 3])
    convb = din('convb', [128, NFC])
    w_down = din('w_down', [DFF, D])
    w_pp = din('w_pp', [256, D]); w_pg = din('w_pg', [D, D])
    out = nc.dram_tensor('out', [NOWN, D], F32, kind="ExternalOutput").ap()

    FM = dscr('FM', [32, 128, S])
    VTM = dscr('VTM', [S, 784])
    NGs = dscr('NG', [S, 24], F32)
    FVR = dscr('FVR', [12, 128, LFV])
    YT = dscr('YT', [8, 128, S])
    X1 = dscr('X1', [S, D], F32)
    X1T = dscr('X1T', [8, 128, S])
    AT = dscr('AT', [NFC, 128, S])
    DBG = dscr('DBG', [128, 4096], F32)

    def toep(h, base, pk, n):
        return bass.AP(FVR.tensor, h * 128 * LFV + OFFD + base, [[LFV - pk, 128], [1, n]])

    with contextlib.ExitStack() as esg:
        P = Prog(nc, esg)

        def SB(es, name, shape, dt):
            return es.enter_context(nc.sbuf_tensor(name, list(shape), dt))

        def PSB(es, name, dt=F32):
            return es.enter_context(nc.psum_tensor(name, [128, 512] if dt == F32 else [128, 1024], dt))

        def mm(o, l, r, st, sp, reads, writes):
            return P.op('pe', lambda e: e.matmul(o, l, r, start=st, stop=sp, skip_group_check=True), reads, writes)

        def tr(o, i_, reads, writes):
            return P.op('pe', lambda e: e.transpose(o, i_, identb[:]), list(reads) + ['identb'], writes)

        def load_cast(es_tmp, dst_fn, src_fn, nrows, ncols, pieces, wname, stg):
            for pi, (c0, c1) in enumerate(pieces):
                ws_, wsn = stg.next()
                P.dma('sp', lambda e, ws_=ws_, c0=c0, c1=c1: e.dma_start(out=ws_[0:nrows, 0:c1 - c0], in_=src_fn(c0, c1)),
                      writes=[wsn])
                ce = 'pool' if pi % 2 == 0 else 'dve'
                P.op(ce, lambda e, ws_=ws_, c0=c0, c1=c1: e.tensor_copy(dst_fn(c0, c1), ws_[0:nrows, 0:c1 - c0]),
                     reads=[wsn], writes=[wname])

        identb = SB(esg, 'identb', [128, 128], BF16)
        zb = SB(esg, 'zb', [128, 512], BF16)
        KCT = SB(esg, 'KCT', [64, 2, 256], BF16)
        VCA = SB(esg, 'VCA', [128, 2, 2, 130], BF16)
        P.dma('pool', lambda e: e.dma_start(out=identb[:], in_=ident_d[:, :]), writes=['identb'])
        P.op('pool', lambda e: e.memset(zb[:], 0.0), writes=['zb'])

        with contextlib.ExitStack() as es:
            ohf = SB(es, 'ohf', [33, LFV], F32)
            ohs = SB(es, 'ohs', [33, LFV], BF16)
            tabs = SB(es, 'tabs', [33, 12], F32)
            ones33 = SB(es, 'ones33', [33, 128], F32)
            tabrep = SB(es, 'tabrep', [33, 12, 128], BF16)
            fvs = Rot('fvs', [SB(es, 'fvs%d' % i, [128, LFV], BF16) for i in range(2)])
            pb = Rot('p0b', [PSB(es, 'p0b%d' % i) for i in range(4)])
            for cch in range(3):
                P.dma('sp', lambda e, cch=cch: e.dma_start(out=ohf[:, cch * 1536:(cch + 1) * 1536],
                                                           in_=ohd[:, cch * 1536:(cch + 1) * 1536]), writes=['ohf'])
            P.dma('sp', lambda e: e.dma_start(out=tabs[:], in_=tabaug[:, :]), writes=['tabs'])
            P.op('dve', lambda e: e.tensor_copy(ohs[:], ohf[:]), reads=['ohf'], writes=['ohs'])
            P.op('pool', lambda e: e.memset(ones33[:], 1.0), writes=['ones33'])
            for h in range(12):
                P.op('dve', lambda e, h=h: e.tensor_scalar(tabrep[:, h, :], ones33[:], tabs[:, h:h + 1], None, ALU.mult),
                     reads=['ones33', 'tabs'], writes=['tabrep'])
            for h in range(12):
                fv, fvn = fvs.next()
                for cch in range(LFV // 512):
                    b_, bn = pb.next()
                    mm(b_[:, :], tabrep[:, h, :], ohs[:, cch * 512:(cch + 1) * 512], True, True, ['ohs', 'tabrep'], [bn])
                    if cch % 2 == 0:
                        P.op('dve', lambda e, b_=b_, fv=fv, cch=cch: e.tensor_copy(fv[:, cch * 512:(cch + 1) * 512], b_[:, :]),
                             reads=[bn], writes=[fvn])
                    else:
                        P.op('act', lambda e, b_=b_, fv=fv, cch=cch: e.copy(fv[:, cch * 512:(cch + 1) * 512], b_[:, :]),
                             reads=[bn], writes=[fvn])
                P.dma('sp', lambda e, fv=fv, h=h: e.dma_start(out=FVR[h], in_=fv[:]), reads=[fvn], writes=['FVR'])
            P.barrier()
            P.emit()
        if DEBUG == 'p0':
            return nc, 1

        with contextlib.ExitStack() as es:
            wf = SB(es, 'wf', [128, 8, 4096], BF16)
            wt = SB(es, 'wt', [128, 8, 792], BF16)
            wst = Rot('wst', [SB(es, 'wst%d' % i, [128, 1024], F32) for i in range(3)])
            xin = Rot('xin', [SB(es, 'xin%d' % i, [128, D], F32) for i in range(2)])
            xbf = Rot('xbf', [SB(es, 'xbf%d' % i, [128, D], BF16) for i in range(2)])
            xTr = Rot('xT', [SB(es, 'xT%d' % i, [128, 8, 512], BF16) for i in range(2)])
            stg = Rot('stg', [SB(es, 'stg%d' % i, [128, 512], BF16) for i in range(4)])
            tms = Rot('tms', [SB(es, 'tms%d' % i, [128, 784], BF16) for i in range(2)])
            ngs = Rot('ngs', [SB(es, 'ngs%d' % i, [128, 24], F32) for i in range(2)])
            pT = Rot('pT', [PSB(es, 'pT%d' % i, BF16) for i in range(2)])
            pacc = Rot('pacc', [PSB(es, 'pacc%d' % i) for i in range(5)])
            for kc in range(8):
                load_cast(es, lambda c0, c1, kc=kc: wf[:, kc, c0:c1], lambda c0, c1, kc=kc: w_fm[kc * 128:(kc + 1) * 128, c0:c1],
                          128, 4096, [(i * 1024, (i + 1) * 1024) for i in range(4)], 'wf', wst)
                load_cast(es, lambda c0, c1, kc=kc: wt[:, kc, c0:c1], lambda c0, c1, kc=kc: w_tm[kc * 128:(kc + 1) * 128, c0:c1],
                          128, 792, [(0, 792)], 'wt', wst)
            for t_ in tms.tiles:
                P.op('pool', lambda e, t_=t_: e.memset(t_[:], 1.0), writes=['tms0', 'tms1'])
            onesb = SB(es, 'onesb', [128, 784], BF16)
            vcs = SB(es, 'vcs', [128, NT], F32)
            P.op('pool', lambda e: e.memset(onesb[:], 1.0), writes=['onesb'])
            P.dma('sp', lambda e: e.dma_start(out=vcs[:], in_=validc[:, :]), writes=['vcs'])
            for tb in range(NQB):
                xT, xTn = xTr.next()
                for i in range(4):
                    tok0 = tb * 512 + i * 128
                    xi, xin_n = xin.next()
                    xb, xbn = xbf.next()
                    pt, ptn = pT.next()
                    P.dma('sp', lambda e, xi=xi, tok0=tok0: e.dma_start(out=xi[:], in_=x[tok0:tok0 + 128, :]), writes=[xin_n])
                    P.op('pool', lambda e, xi=xi, xb=xb: e.tensor_copy(xb[:], xi[:]), reads=[xin_n], writes=[xbn])
                    for kc in range(8):
                        tr(pt[:, kc * 128:(kc + 1) * 128], xb[:, kc * 128:(kc + 1) * 128], [xbn], [ptn])
                    P.op('act', lambda e, pt=pt, xT=xT, i=i: e.copy(
                        xT[:, :, i * 128:(i + 1) * 128], pt[:].rearrange("p (a b) -> p a b", b=128)), reads=[ptn], writes=[xTn])
                for oc in range(32 if tb >= QB0 else 8):
                    pa, pan = pacc.next()
                    for kc in range(8):
                        mm(pa[:, :], wf[:, kc, oc * 128:(oc + 1) * 128], xT[:, kc, :], kc == 0, kc == 7, ['wf', xTn], [pan])
                    st, stn = stg.next()
                    if oc < 8:
                        P.op('dve', lambda e, st=st, pa=pa: e.tensor_copy(st[:], pa[:]), reads=[pan], writes=[stn])
                    elif oc < 16:
                        P.op('dve', lambda e, st=st, pa=pa: e.tensor_scalar(st[:], pa[:], 0.125, None, ALU.mult),
                             reads=[pan], writes=[stn])
                    else:
                        P.op('act', lambda e, st=st, pa=pa: e.activation(st[:], pa[:], AF.Sigmoid), reads=[pan], writes=[stn])
                    P.dma('sp', lambda e, st=st, oc=oc, tb=tb: e.dma_start(out=FM[oc, :, tb * 512:(tb + 1) * 512], in_=st[:]),
                          reads=[stn], writes=['FM%d' % oc])
                for i in range(4):
                    tok0 = tb * 512 + i * 128
                    pa, pan = pacc.next()
                    pb_, pbn = pacc.next()
                    for kc in range(8):
                        mm(pa[:, :], xT[:, kc, i * 128:(i + 1) * 128], wt[:, kc, 0:512], kc == 0, kc == 7, ['wt', xTn], [pan])
                    for kc in range(8):
                        mm(pb_[:, 0:280], xT[:, kc, i * 128:(i + 1) * 128], wt[:, kc, 512:792], kc == 0, kc == 7, ['wt', xTn], [pbn])
                    tm, tmn = tms.next()
                    ng, ngn = ngs.next()
                    P.op('dve', lambda e, tm=tm, pa=pa: e.tensor_copy(
                        tm[:, 0:264].rearrange("p (a b) -> p a b", b=66)[:, :, 0:64],
                        pa[:, 0:256].rearrange("p (a b) -> p a b", b=64)), reads=[pan], writes=[tmn])
                    P.op('dve', lambda e, tm=tm, pa=pa: e.tensor_copy(
                        tm[:, 264:524].rearrange("p (a b) -> p a b", b=130)[:, :, 0:128],
                        pa[:, 256:512].rearrange("p (a b) -> p a b", b=128)), reads=[pan, tmn], writes=[tmn])
                    P.op('act', lambda e, tm=tm, pb_=pb_: e.copy(
                        tm[:, 524:784].rearrange("p (a b) -> p a b", b=130)[:, :, 0:128],
                        pb_[:, 0:256].rearrange("p (a b) -> p a b", b=128)), reads=[pbn, tmn], writes=[tmn])
                    P.op('act', lambda e, ng=ng, pb_=pb_: e.copy(ng[:], pb_[:, 256:280]), reads=[pbn], writes=[ngn])
                    tix = tb * 4 + i
                    P.op('dve', lambda e, tm=tm, tix=tix: e.tensor_scalar(
                        tm[:, 0:264].rearrange("p (a b) -> p a b", b=66)[:, :, 64:65],
                        onesb[:, 0:264].rearrange("p (a b) -> p a b", b=66)[:, :, 64:65], vcs[:, tix:tix + 1], None, ALU.mult),
                        reads=['onesb', 'vcs', tmn], writes=[tmn])
                    P.op('dve', lambda e, tm=tm, tix=tix: e.tensor_scalar(
                        tm[:, 264:784].rearrange("p (a b) -> p a b", b=130)[:, :, 128:129],
                        onesb[:, 264:784].rearrange("p (a b) -> p a b", b=130)[:, :, 128:129], vcs[:, tix:tix + 1], None, ALU.mult),
                        reads=['onesb', 'vcs', tmn], writes=[tmn])
                    P.dma('sp', lambda e, tm=tm, tok0=tok0: e.dma_start(out=VTM[tok0:tok0 + 128, :], in_=tm[:]),
                          reads=[tmn], writes=['VTM'])
                    P.dma('sp', lambda e, ng=ng, tok0=tok0: e.dma_start(out=NGs[tok0:tok0 + 128, :], in_=ng[:]),
                          reads=[ngn], writes=['NG'])
            P.barrier()
            P.emit()
        if DEBUG == 'p1':
            return nc, 1
        FMALL = ['FM%d' % i for i in range(32)]

        with contextlib.ExitStack() as es:
            kin = SB(es, 'kin', [128, S], BF16)
            vin = SB(es, 'vin', [128, S], BF16)
            w1s = [SB(es, 'w1s%d' % i, [128, 32, 256], BF16) for i in range(2)]
            w1st = Rot('w1st', [SB(es, 'w1st%d' % i, [128, 8, 256], F32) for i in range(2)])
            w2f = SB(es, 'w2f', [128, 2, 2, 64], F32)
            w2b = SB(es, 'w2b', [128, 2, 2, 64], BF16)
            pef = SB(es, 'pef', [64, 2, 32], F32)
            peb16 = SB(es, 'peb16', [64, 2, 32], BF16)
            pebias = SB(es, 'pebias', [128, 4], F32)
            hT = [SB(es, 'hT%d' % i, [128, 256], BF16) for i in range(8)]
            cvs = SB(es, 'cvs', [128, 2], F32)
            msf = SB(es, 'msf', [128, 2, 64], F32)
            pc = Rot('pc', [PSB(es, 'pc%d' % i) for i in range(4)])
            P.dma('sp', lambda e: e.dma_start(out=kin[:], in_=FM[2]), reads=FMALL, writes=['kin'])
            P.dma('sp', lambda e: e.dma_start(out=vin[:], in_=FM[3]), reads=FMALL, writes=['vin'])
            for kv, w1 in enumerate((w1k, w1v)):
                src = w1.rearrange("(l d) h -> d l h", d=64)
                for q4 in range(4):
                    ws_, wsn = w1st.next()
                    for half in range(2):
                        P.dma('sp', lambda e, ws_=ws_, half=half, q4=q4, src=src: e.dma_start(
                            out=ws_[half * 64:(half + 1) * 64, :, :], in_=src[:, q4 * 8:(q4 + 1) * 8, :]), writes=[wsn])
                    P.op('pool' if q4 % 2 else 'dve', lambda e, ws_=ws_, kv=kv, q4=q4: e.tensor_copy(
                        w1s[kv][:, q4 * 8:(q4 + 1) * 8, :], ws_[:]), reads=[wsn], writes=['w1s%d' % kv])
            for kv, w2 in enumerate((w2k, w2v)):
                P.dma('sp', lambda e, kv=kv, w2=w2: e.dma_start(out=w2f[:, kv, :, :], in_=w2.rearrange("(c p) d -> p c d", p=128)),
                      writes=['w2f'])
            P.op('dve', lambda e: e.tensor_copy(w2b[:], w2f[:]), reads=['w2f'], writes=['w2b'])
            P.dma('sp', lambda e: e.dma_start(out=pef[:, 0, :], in_=pekT[:, :]), writes=['pef'])
            P.dma('sp', lambda e: e.dma_start(out=pef[:, 1, :], in_=pevT[:, :]), writes=['pef'])
            P.op('dve', lambda e: e.tensor_copy(peb16[:], pef[:]), reads=['pef'], writes=['peb16'])
            P.dma('sp', lambda e: e.dma_start(out=cvs[:], in_=cvalid[:, :]), writes=['cvs'])
            P.dma('sp', lambda e: e.dma_start(out=msf[:], in_=mslc.rearrange("(c p) j -> p c j", p=128)), writes=['msf'])
            for t_ in hT:
                P.op('pool', lambda e, t_=t_: e.memset(t_[:], 0.0), writes=['hT'])
            P.op('pool', lambda e: e.memset(KCT[:], 0.0), writes=['KCT'])
            P.op('pool', lambda e: e.memset(VCA[:], 0.0), writes=['VCA'])
            for kv in range(2):
                for hc in range(2):
                    pp_, ppn = pc.next()
                    for l in range(32):
                        mm(pp_[:, 0:1], w1s[kv][0:64, l, hc * 128:(hc + 1) * 128], peb16[0:64, kv, l:l + 1], l == 0, l == 31,
                           ['w1s%d' % kv, 'peb16'], [ppn])
                    P.op('dve', lambda e, pp_=pp_, kv=kv, hc=hc: e.tensor_copy(pebias[:, kv * 2 + hc:kv * 2 + hc + 1], pp_[:, 0:1]),
                         reads=[ppn], writes=['pebias'])
            for kv, src_t, srcn in ((0, kin, 'kin'), (1, vin, 'vin')):
                for g in range(2):
                    for hc in range(2):
                        pp_, ppn = pc.next()
                        for l in range(32):
                            if l < 16:
                                rhs = src_t[64 * g:64 * g + 64, :].rearrange("p (n s) -> p n s", s=16)[:, 0:255, l]
                            else:
                                rhs = src_t[64 * g:64 * g + 64, 16:S].rearrange("p (n s) -> p n s", s=16)[:, 0:255, l - 16]
                            mm(pp_[:, 0:255], w1s[kv][64 * g:64 * g + 64, l, hc * 128:(hc + 1) * 128], rhs, l == 0, l == 31,
                               ['w1s%d' % kv, srcn], [ppn])
                        ht = hT[kv * 4 + g * 2 + hc]
                        P.op('act', lambda e, ht=ht, pp_=pp_, kv=kv, hc=hc: e.activation(
                            ht[:, 0:255], pp_[:, 0:255], AF.Gelu_apprx_tanh, bias=pebias[:, kv * 2 + hc:kv * 2 + hc + 1]),
                            reads=[ppn, 'pebias', 'hT'], writes=['hT'])
            for g in range(2):
                pp_, ppn = pc.next()
                for hc in range(2):
                    mm(pp_[0:64, 0:255], w2b[:, 0, hc, :], hT[g * 2 + hc][:, 0:255], hc == 0, hc == 1, ['w2b', 'hT'], [ppn])
                P.op('dve', lambda e, pp_=pp_, g=g: e.tensor_copy(KCT[0:64, g, 0:255], pp_[0:64, 0:255]),
                     reads=[ppn, 'KCT'], writes=['KCT'])
                for nt in range(2):
                    pp2, pp2n = pc.next()
                    for hc in range(2):
                        mm(pp2[:, 0:64], hT[4 + g * 2 + hc][:, nt * 128:(nt + 1) * 128], w2b[:, 1, hc, :], hc == 0, hc == 1,
                           ['w2b', 'hT'], [pp2n])
                    P.op('dve', lambda e, pp2=pp2, g=g, nt=nt: e.tensor_scalar(
                        VCA[:, nt, g, 0:64], pp2[:, 0:64], cvs[:, nt:nt + 1], None, ALU.mult),
                        reads=[pp2n, 'cvs', 'VCA'], writes=['VCA'])
                    P.op('dve', lambda e, g=g, nt=nt: e.tensor_copy(VCA[:, nt, g, 64:65], cvs[:, nt:nt + 1]),
                         reads=['cvs', 'VCA'], writes=['VCA'])
                    P.op('dve', lambda e, g=g, nt=nt: e.tensor_copy(VCA[:, nt, g, 65:129], msf[:, nt, :]),
                         reads=['msf', 'VCA'], writes=['VCA'])
            if DEBUG == 'p2c':
                dt_ = SB(es, 'dbgt', [128, 4096], F32)
                P.op('pool', lambda e: e.memset(dt_[:], 0.0), writes=['dbgt'])
                P.op('dve', lambda e: e.tensor_copy(dt_[0:64, 0:512], KCT[:].rearrange("p a b -> p (a b)")), reads=['KCT', 'dbgt'], writes=['dbgt'])
                P.op('dve', lambda e: e.tensor_copy(dt_[:, 512:1032], VCA[:].rearrange("p a b c -> p (a b c)")), reads=['VCA', 'dbgt'], writes=['dbgt'])
                P.dma('sp', lambda e: e.dma_start(out=DBG[:, :], in_=dt_[:]), reads=['dbgt'], writes=['DBG'])
            P.barrier()
            P.emit()
        if DEBUG == 'p2c':
            return nc, 1

        def attn_pass(Sb, ACCb, PTr, accr, units, nq, keys, w):
            nu = len(units)
            nqt = nq // 128
            per = 512 // w
            nslots = nu * nqt
            nbank = (nslots + per - 1) // per
            for b in range(nbank):
                mm(ACCb[b][:, :], zb[:, 0:128], zb[:, 0:512], True, True, ['zb'], ['ACC'])
            pend = None

            def do_pv(items, last):
                for (pt, ptn, key, u) in items:
                    for i in range(nqt):
                        s_ = i * nu + u
                        mm(ACCb[s_ // per][:, (s_ % per) * w:(s_ % per) * w + w], pt[:, i * 128:(i + 1) * 128], key['v'](u),
                           False, last, [ptn] + key['reads'], ['ACC'])
            cnt = 0
            for ki, key in enumerate(keys):
                items = []
                for u in range(nu):
                    sbk = Sb[cnt % 4]
                    sbn = 'S%d' % (cnt % 4)
                    cnt += 1
                    bias = key['bias'](u) if key['bias'] is not None else None
                    mm(sbk[:, 0:nq], key['lhsT'](u), units[u]['rhs'], True, bias is None, [units[u]['res']] + key['reads'], [sbn])
                    if bias is not None:
                        mm(sbk[:, 0:nq], identb[:], bias[0], False, True, ['identb', bias[1]], [sbn])
                    pt, ptn = PTr.next()
                    P.op('act', lambda e, pt=pt, sbk=sbk: e.activation(pt[:, 0:nq], sbk[:, 0:nq], AF.Exp), reads=[sbn], writes=[ptn])
                    items.append((pt, ptn, key, u))
                if pend is not None:
                    do_pv(pend, False)
                pend = items
            do_pv(pend, True)
            asb, asbn = accr.next()
            for b in range(nbank):
                used = min(per, nslots - b * per) * w
                P.op('dve', lambda e, b=b, used=used, asb=asb: e.tensor_copy(asb[:, b, 0:used], ACCb[b][:, 0:used]),
                     reads=['ACC'], writes=[asbn])
            return asb, asbn, per, nslots, nbank

        def recip_sums(asb, asbn, rs, per, nslots, nbank, w, sumcol):
            for b in range(nbank):
                nsb = min(per, nslots - b * per)
                P.op('dve', lambda e, b=b, nsb=nsb: e.tensor_scalar(
                    rs[:, b * per:b * per + nsb], asb[:, b, 0:nsb * w].rearrange("p (s c) -> p s c", c=w)[:, :, sumcol],
                    1e-20, None, ALU.max), reads=[asbn, 'rs'], writes=['rs'])
            P.op('dve', lambda e: e.reciprocal(rs[:, 0:nslots], rs[:, 0:nslots]), reads=['rs'], writes=['rs'])

        with contextlib.ExitStack() as es:
            Ub = SB(es, 'Ub', [128, 8, 6, 512], BF16)
            Utf = SB(es, 'Utf', [128, 3072], F32)
            Utb = SB(es, 'Utb', [128, 6, 512], BF16)
            Wc = SB(es, 'Wc', [128, 3, 512], BF16)
            KS = SB(es, 'KS', [128, 2, S], BF16)
            KW = SB(es, 'KW', [64, 2, S], BF16)
            VSW = SB(es, 'VSW', [128, NT, 264], BF16)
            amk = SB(es, 'amk', [128, NT * 64], F32)
            QSr = [Rot('QS%d' % u, [SB(es, 'QS%d_%d' % (u, i), [128, 512], BF16) for i in range(2)]) for u in range(4)]
            PTr = Rot('PT', [SB(es, 'PT%d' % i, [128, 512], BF16) for i in range(8)])
            cbr = Rot('cb', [SB(es, 'cb%d' % i, [128, 512], BF16) for i in range(4)])
            accr = Rot('asb', [SB(es, 'asb%d' % i, [128, 3, 512], F32) for i in range(2)])
            rs = SB(es, 'rs', [128, 16], F32)
            coef = SB(es, 'coef', [128, 16], F32)
            gsb = Rot('gsb', [SB(es, 'gsb%d' % i, [128, 4, 24], F32) for i in range(2)])
            ynsa = SB(es, 'ynsa', [128, 4, 512], F32)
            ynb = SB(es, 'ynb', [128, 4, 512], BF16)
            psl = SB(es, 'psl', [128, 4, 64], F32)
            sc = SB(es, 'sc', [128, 64], F32)
            sc2 = SB(es, 'sc2', [128, 64], F32)
            m8 = SB(es, 'm8', [128, 16], F32)
            nmb = SB(es, 'nmb', [128, 64], BF16)
            nmT = Rot('nmT', [SB(es, 'nmT%d' % i, [64, 512], BF16) for i in range(2)])
            ystg = Rot('ystg', [SB(es, 'ystg%d' % i, [128, 512], BF16) for i in range(2)])
            Sb = [PSB(es, 'Sb%d' % i) for i in range(4)]
            ACCb = [PSB(es, 'ACCb%d' % i) for i in range(3)]
            pm = PSB(es, 'pm', BF16)
            for cch in range(3):
                P.dma('sp', lambda e, cch=cch: e.dma_start(out=Utf[:, cch * 1024:(cch + 1) * 1024],
                                                           in_=utmpl[:, cch * 1024:(cch + 1) * 1024]), writes=['Utf'])
            P.op('dve', lambda e: e.tensor_copy(Utb[:].rearrange("p a b -> p (a b)"), Utf[:]), reads=['Utf'], writes=['Utb'])
            for h in range(8):
                P.op('pool', lambda e, h=h: e.tensor_copy(Ub[:, h, :, :], Utb[:]), reads=['Utb'], writes=['Ub%d' % h])
                for (m, i, base) in [(1, 0, 0), (2, 1, 0), (3, 2, 0), (4, 3, 0), (0, 0, 128), (1, 1, 128), (2, 2, 128), (3, 3, 128),
                                     (5, 0, 128)]:
                    P.dma('sp', lambda e, h=h, m=m, i=i, base=base: e.dma_start(
                        out=Ub[:, h, m, i * 128:(i + 1) * 128], in_=toep(h, base, 1, 128)), reads=['FVR'], writes=['Ub%d' % h])
            P.dma('sp', lambda e: e.dma_start(out=Utf[:, 0:1536], in_=wctmpl[:, :]), reads=['Utb'], writes=['Utf'])
            P.op('dve', lambda e: e.tensor_copy(Wc[:].rearrange("p a b -> p (a b)"), Utf[:, 0:1536]), reads=['Utf'], writes=['Wc'])
            for g in range(2):
                P.dma('sp', lambda e, g=g: e.dma_start(out=KS[0:64, g, :], in_=FM[0, 64 * g:64 * g + 64, :]), reads=FMALL, writes=['KS'])
                P.dma('sp', lambda e, g=g: e.dma_start(out=KW[0:64, g, :], in_=FM[1, 64 * g:64 * g + 64, :]), reads=FMALL, writes=['KW'])
            for hf in range(2):
                P.dma('sp', lambda e, hf=hf: e.dma_start(out=Utf[64:128, 0:2048], in_=eall[:, hf * 2048:(hf + 1) * 2048]),
                      reads=['Wc'], writes=['Utf'])
                for g in range(2):
                    P.op('dve' if g else 'pool', lambda e, g=g, hf=hf: e.tensor_copy(
                        KS[64:128, g, hf * 2048:(hf + 1) * 2048], Utf[64:128, 0:2048]), reads=['Utf', 'KS'], writes=['KS'])
            for q4 in range(4):
                P.dma('sp', lambda e, q4=q4: e.dma_start(
                    out=VSW[:, q4 * 8:(q4 + 1) * 8, :],
                    in_=VTM[q4 * 1024:(q4 + 1) * 1024, 0:264].rearrange("(t p) c -> p t c", p=128)), reads=['VTM'], writes=['VSW'])
            P.dma('sp', lambda e: e.dma_start(out=amk[:], in_=addmask[:, :]), writes=['amk'])

            for qb in range(QB0, NQB):
                q0 = qb * 512
                nq = 512
                G0 = qb * 4
                gs_, gsn = gsb.next()
                P.dma('sp', lambda e, gs_=gs_, q0=q0: e.dma_start(
                    out=gs_[:], in_=NGs[q0:q0 + 512, :].rearrange("(i p) c -> p i c", p=128)), reads=['NG'], writes=[gsn])
                P.op('act', lambda e, gs_=gs_: e.activation(gs_[:], gs_[:], AF.Sigmoid), reads=[gsn], writes=[gsn])
                for g in range(2):
                    units = []
                    for u in range(4):
                        qs, qsn = QSr[u].next()
                        P.dma('sp', lambda e, qs=qs, u=u, g=g, q0=q0: e.dma_start(
                            out=qs[0:64, :], in_=FM[8 + u, 64 * g:64 * g + 64, q0:q0 + 512]), reads=FMALL, writes=[qsn])
                        units.append(dict(t=qs, res=qsn))
                    for p2 in range(2):
                        us = [dict(rhs=units[2 * p2 + ul]['t'][0:64, :], res=units[2 * p2 + ul]['res']) for ul in range(2)]
                        keys = []
                        for nt in range(2):
                            typ = cmp_tile_type(q0, nq, nt)
                            if typ == 'skip':
                                continue
                            bias_tiles = {}
                            if typ == 'bias':
                                for ul in range(2):
                                    h = 4 * g + 2 * p2 + ul
                                    cb, cbn = cbr.next()
                                    P.dma('sp', lambda e, cb=cb, h=h, q0=q0, nt=nt: e.dma_start(
                                        out=cb[:, :], in_=toep(h, q0 - 16 * 128 * nt - 31, 16, 512)), reads=['FVR'], writes=[cbn])
                                    bias_tiles[ul] = (cb[:, :], cbn)
                            keys.append(dict(
                                lhsT=lambda u, nt=nt, g=g: KCT[0:64, g, nt * 128:(nt + 1) * 128],
                                bias=(lambda u, bt=bias_tiles: bt[u]) if typ == 'bias' else None,
                                v=lambda u, nt=nt, g=g: VCA[:, nt, g, 0:129], reads=['KCT', 'VCA']))
                        asb, asbn, per, nslots, nbank = attn_pass(Sb, ACCb, PTr, accr, us, nq, keys, 129)
                        recip_sums(asb, asbn, rs, per, nslots, nbank, 129, 64)
                        gv = gs_[:, :, 12 * g + 6 * p2:12 * g + 6 * p2 + 6].rearrange("p i (u c) -> p i u c", c=3)[:, :, :, 0]
                        P.op('dve', lambda e, gv=gv: e.tensor_tensor(coef[:, 0:8].rearrange("p (i u) -> p i u", u=2),
                                                                     rs[:, 0:8].rearrange("p (i u) -> p i u", u=2), gv, ALU.mult),
                             reads=['rs', gsn, 'coef'], writes=['coef'])
                        for i in range(4):
                            for ul in range(2):
                                s_ = i * 2 + ul
                                h = 4 * g + 2 * p2 + ul
                                b_, c_ = s_ // per, (s_ % per) * 129
                                P.op('dve', lambda e, asb=asb, b_=b_, c_=c_, i=i, h=h, s_=s_: e.tensor_scalar(
                                    ynsa[:, i, h * 64:(h + 1) * 64], asb[:, b_, c_:c_ + 64], coef[:, s_:s_ + 1], None, ALU.mult),
                                    reads=[asbn, 'coef', 'ynsa'], writes=['ynsa'])
                                if p2 == 0 and ul == 0:
                                    P.op('dve', lambda e, asb=asb, b_=b_, c_=c_, i=i, s_=s_: e.tensor_scalar(
                                        psl[:, i, :], asb[:, b_, c_ + 65:c_ + 129], rs[:, s_:s_ + 1], None, ALU.mult),
                                        reads=[asbn, 'rs', 'psl'], writes=['psl'])
                                else:
                                    P.op('dve', lambda e, asb=asb, b_=b_, c_=c_, i=i, s_=s_: e.scalar_tensor_tensor(
                                        out=psl[:, i, :], in0=asb[:, b_, c_ + 65:c_ + 129], scalar=rs[:, s_:s_ + 1], in1=psl[:, i, :],
                                        op0=ALU.mult, op1=ALU.add), reads=[asbn, 'rs', 'psl'], writes=['psl'])
                    nm, nmn = nmT.next()
                    for i in range(4):
                        gt = G0 + i
                        P.op('dve', lambda e, i=i, gt=gt: e.tensor_tensor(sc[:], psl[:, i, :], amk[:, gt * 64:(gt + 1) * 64], ALU.add),
                             reads=['psl', 'amk', 'sc'], writes=['sc'])
                        P.op('dve', lambda e: e.max(m8[:, 0:8], sc[:]), reads=['sc', 'm8'], writes=['m8'])
                        P.op('dve', lambda e: e.match_replace(sc2[:], m8[:, 0:8], sc[:], -1e30), reads=['sc', 'm8', 'sc2'], writes=['sc2'])
                        P.op('dve', lambda e: e.max(m8[:, 8:16], sc2[:]), reads=['sc2', 'm8'], writes=['m8'])
                        P.op('dve', lambda e: e.tensor_scalar(nmb[:], sc[:], m8[:, 15:16], NEG, ALU.is_lt, ALU.mult),
                             reads=['sc', 'm8', 'nmb'], writes=['nmb'])
                        tr(pm[0:64, i * 128:(i + 1) * 128], nmb[:, :], ['nmb'], ['pm'])
                    P.op('act', lambda e, nm=nm: e.copy(nm[:, :], pm[0:64, 0:512]), reads=['pm'], writes=[nmn])
                    for u in range(4):
                        P.dma('sp', lambda e, u=u, nm=nm, units=units: e.dma_start(out=units[u]['t'][64:128, :], in_=nm[:, :]),
                              reads=[nmn, units[u]['res']], writes=[units[u]['res']])
                    for br in (1, 2):
                        keys = []
                        if br == 1:
                            us = [dict(rhs=units[u]['t'][:, :], res=units[u]['res']) for u in range(4)]
                            for kt in range(G0 + 4):
                                m = kt - (G0 - 1)
                                keys.append(dict(
                                    lhsT=lambda u, kt=kt, g=g: KS[:, g, kt * 128:(kt + 1) * 128],
                                    bias=(lambda u, m=m, g=g: (Ub[:, 4 * g + u, m, :], 'Ub%d' % (4 * g + u))) if m >= 0 else None,
                                    v=lambda u, kt=kt, g=g: VSW[:, kt, g * 66:g * 66 + 65], reads=['KS', 'VSW']))
                        else:
                            us = [dict(rhs=units[u]['t'][0:64, :], res=units[u]['res']) for u in range(4)]
                            for jj in range(8):
                                kt = G0 - 4 + jj
                                if kt < 0:
                                    continue
                                if jj < 3:
                                    bf_ = lambda u, jj=jj: (Wc[:, jj, :], 'Wc')
                                elif jj == 3:
                                    bf_ = lambda u, g=g: (Ub[:, 4 * g + u, 5, :], 'Ub%d' % (4 * g + u))
                                else:
                                    bf_ = lambda u, g=g, jj=jj: (Ub[:, 4 * g + u, jj - 3, :], 'Ub%d' % (4 * g + u))
                                keys.append(dict(
                                    lhsT=lambda u, kt=kt, g=g: KW[0:64, g, kt * 128:(kt + 1) * 128], bias=bf_,
                                    v=lambda u, kt=kt, g=g: VSW[:, kt, 132 + g * 66:132 + g * 66 + 65], reads=['KW', 'VSW']))
                        asb, asbn, per, nslots, nbank = attn_pass(Sb, ACCb, PTr, accr, us, nq, keys, 65)
                        recip_sums(asb, asbn, rs, per, nslots, nbank, 65, 64)
                        gv = gs_[:, :, 12 * g:12 * g + 12].rearrange("p i (u c) -> p i u c", c=3)[:, :, :, br]
                        P.op('dve', lambda e, gv=gv: e.tensor_tensor(coef[:, 0:16].rearrange("p (i u) -> p i u", u=4),
                                                                     rs[:, 0:16].rearrange("p (i u) -> p i u", u=4), gv, ALU.mult),
                             reads=['rs', gsn, 'coef'], writes=['coef'])
                        for i in range(4):
                            for u in range(4):
                                s_ = i * 4 + u
                                h = 4 * g + u
                                b_, c_ = s_ // per, (s_ % per) * 65
                                P.op('dve', lambda e, asb=asb, b_=b_, c_=c_, i=i, h=h, s_=s_: e.scalar_tensor_tensor(
                                    out=ynsa[:, i, h * 64:(h + 1) * 64], in0=asb[:, b_, c_:c_ + 64], scalar=coef[:, s_:s_ + 1],
                                    in1=ynsa[:, i, h * 64:(h + 1) * 64], op0=ALU.mult, op1=ALU.add),
                                    reads=[asbn, 'coef', 'ynsa'], writes=['ynsa'])
                P.op('pool', lambda e: e.tensor_copy(ynb[:], ynsa[:]), reads=['ynsa', 'ynb'], writes=['ynb'])
                for c in range(4):
                    for i in range(4):
                        tr(pm[:, i * 128:(i + 1) * 128], ynb[:, i, c * 128:(c + 1) * 128], ['ynb'], ['pm'])
                    ys, ysn = ystg.next()
                    P.op('act', lambda e, ys=ys: e.copy(ys[:, :], pm[:, 0:512]), reads=['pm'], writes=[ysn])
                    P.dma('sp', lambda e, ys=ys, c=c, q0=q0: e.dma_start(out=YT[c, :, q0:q0 + 512], in_=ys[:, :]),
                          reads=[ysn], writes=['YT'])
            P.barrier()
            P.emit()
        if DEBUG == 'p2a':
            return nc, 1

        with contextlib.ExitStack() as es:
            Ud = SB(es, 'Ud', [128, 4, 6, 512], BF16)
            Utf = SB(es, 'Utf2', [128, 3072], F32)
            Utb = SB(es, 'Utb2', [128, 6, 512], BF16)
            KD = SB(es, 'KD', [128, 4, S], BF16)
            VD = SB(es, 'VD', [128, NT, 520], BF16)
            QAr = Rot('QA', [SB(es, 'QA%d' % i, [128, 512], BF16) for i in range(2)])
            QBr = Rot('QB', [SB(es, 'QB%d' % i, [128, 512], BF16) for i in range(2)])
            PTr = Rot('PT', [SB(es, 'PTd%d' % i, [128, 512], BF16) for i in range(8)])
            accr = Rot('asb', [SB(es, 'asbd%d' % i, [128, 3, 512], F32) for i in range(2)])
            rs = SB(es, 'rsd', [128, 16], F32)
            cf = SB(es, 'cfd', [128, 16], F32)
            lamt = SB(es, 'lamt', [128, 256], F32)
            lpr = SB(es, 'lpr', [128, 128], F32)
            lsm = SB(es, 'lsm', [128, 4], F32)
            neglam = SB(es, 'neglam', [128, 1], F32)
            subg = SB(es, 'subg', [128, 128], F32)
            tmp1 = SB(es, 'tmp1', [128, 4, 128], F32)
            od = SB(es, 'od', [128, 4, 128], F32)
            junk = SB(es, 'junk', [128, 128], F32)
            ss = SB(es, 'ssd', [128, 16], F32)
            yd = SB(es, 'yd', [128, 4, 512], F32)
            ydb = SB(es, 'ydb', [128, 4, 512], BF16)
            ystg = Rot('ystg', [SB(es, 'ystgd%d' % i, [128, 512], BF16) for i in range(2)])
            Sb = [PSB(es, 'Sbd%d' % i) for i in range(4)]
            ACCb = [PSB(es, 'ACCd%d' % i) for i in range(3)]
            pm = PSB(es, 'pmd', BF16)
            for cch in range(3):
                P.dma('sp', lambda e, cch=cch: e.dma_start(out=Utf[:, cch * 1024:(cch + 1) * 1024],
                                                           in_=utmpl[:, cch * 1024:(cch + 1) * 1024]), writes=['Utf'])
            P.op('dve', lambda e: e.tensor_copy(Utb[:].rearrange("p a b -> p (a b)"), Utf[:]), reads=['Utf'], writes=['Utb'])
            for hd in range(4):
                P.op('pool', lambda e, hd=hd: e.tensor_copy(Ud[:, hd, :, :], Utb[:]), reads=['Utb'], writes=['Ud%d' % hd])
                for (m, i, base) in [(1, 0, 0), (2, 1, 0), (3, 2, 0), (4, 3, 0), (0, 0, 128), (1, 1, 128), (2, 2, 128), (3, 3, 128)]:
                    P.dma('sp', lambda e, hd=hd, m=m, i=i, base=base: e.dma_start(
                        out=Ud[:, hd, m, i * 128:(i + 1) * 128], in_=toep(8 + hd, base, 1, 128)), reads=['FVR'], writes=['Ud%d' % hd])
                P.dma('sp', lambda e, hd=hd: e.dma_start(out=KD[:, hd, :], in_=FM[4 + hd]), reads=FMALL, writes=['KD'])
            for q4 in range(4):
                P.dma('sp', lambda e, q4=q4: e.dma_start(
                    out=VD[:, q4 * 8:(q4 + 1) * 8, :],
                    in_=VTM[q4 * 1024:(q4 + 1) * 1024, 264:784].rearrange("(t p) c -> p t c", p=128)), reads=['VTM'], writes=['VD'])
            for i_ in range(2):
                P.op('pool', lambda e, t_=QAr.tiles[i_]: e.memset(t_[64:128, :], 0.0), writes=['QA%d' % i_])
                P.op('pool', lambda e, t_=QBr.tiles[i_]: e.memset(t_[0:64, :], 0.0), writes=['QB%d' % i_])
            P.dma('sp', lambda e: e.dma_start(out=lamt[:], in_=lam_in.partition_broadcast(128)), writes=['lamt'])
            P.dma('sp', lambda e: e.dma_start(out=subg[:], in_=subln.partition_broadcast(128)), writes=['subg'])
            P.op('dve', lambda e: e.tensor_tensor(lpr[:, 0:64], lamt[:, 0:64], lamt[:, 64:128], ALU.mult), reads=['lamt'], writes=['lpr'])
            P.op('dve', lambda e: e.tensor_tensor(lpr[:, 64:128], lamt[:, 128:192], lamt[:, 192:256], ALU.mult),
                 reads=['lamt', 'lpr'], writes=['lpr'])
            P.op('dve', lambda e: e.reduce_sum(lsm[:, 0:1], lpr[:, 0:64], AX.X), reads=['lpr'], writes=['lsm'])
            P.op('dve', lambda e: e.reduce_sum(lsm[:, 1:2], lpr[:, 64:128], AX.X), reads=['lpr', 'lsm'], writes=['lsm'])
            P.op('act', lambda e: e.activation(lsm[:, 2:4], lsm[:, 0:2], AF.Exp), reads=['lsm'], writes=['lsm'])
            P.op('dve', lambda e: e.tensor_tensor(neglam[:], lsm[:, 3:4], lsm[:, 2:3], ALU.subtract), reads=['lsm'], writes=['neglam'])
            P.op('dve', lambda e: e.tensor_scalar(neglam[:], neglam[:], -0.2, None, ALU.add), reads=['neglam'], writes=['neglam'])

            for qb in range(QB0, NQB):
                q0 = qb * 512
                nq = 512
                G0 = qb * 4
                for hd in range(4):
                    qa, qan = QAr.next()
                    qb_, qbn = QBr.next()
                    P.dma('sp', lambda e, qa=qa, hd=hd, q0=q0: e.dma_start(out=qa[0:64, :], in_=FM[12 + hd, 0:64, q0:q0 + 512]),
                          reads=FMALL + [qan], writes=[qan])
                    P.dma('sp', lambda e, qb_=qb_, hd=hd, q0=q0: e.dma_start(out=qb_[64:128, :], in_=FM[12 + hd, 64:128, q0:q0 + 512]),
                          reads=FMALL + [qbn], writes=[qbn])
                    us = [dict(rhs=qa[:, :], res=qan), dict(rhs=qb_[:, :], res=qbn)]
                    keys = []
                    for kt in range(G0 + 4):
                        m = kt - (G0 - 1)
                        keys.append(dict(
                            lhsT=lambda u, kt=kt, hd=hd: KD[:, hd, kt * 128:(kt + 1) * 128],
                            bias=(lambda u, m=m, hd=hd: (Ud[:, hd, m, :], 'Ud%d' % hd)) if m >= 0 else None,
                            v=lambda u, kt=kt, hd=hd: VD[:, kt, hd * 130:hd * 130 + 129], reads=['KD', 'VD']))
                    asb, asbn, per, nslots, nbank = attn_pass(Sb, ACCb, PTr, accr, us, nq, keys, 129)
                    recip_sums(asb, asbn, rs, per, nslots, nbank, 129, 128)
                    P.op('dve', lambda e: e.tensor_scalar(cf[:, 0:8], rs[:, 0:8], neglam[:, 0:1], None, ALU.mult),
                         reads=['rs', 'neglam', 'cf'], writes=['cf'])
                    for i in range(4):
                        s1, s2 = 2 * i, 2 * i + 1
                        b1, c1 = s1 // per, (s1 % per) * 129
                        b2, c2 = s2 // per, (s2 % per) * 129
                        P.op('dve', lambda e, asb=asb, b1=b1, c1=c1, i=i, s1=s1: e.tensor_scalar(
                            tmp1[:, i, :], asb[:, b1, c1:c1 + 128], rs[:, s1:s1 + 1], None, ALU.mult),
                            reads=[asbn, 'rs', 'tmp1'], writes=['tmp1'])
                        P.op('dve', lambda e, asb=asb, b2=b2, c2=c2, i=i, s2=s2: e.scalar_tensor_tensor(
                            out=od[:, i, :], in0=asb[:, b2, c2:c2 + 128], scalar=cf[:, s2:s2 + 1], in1=tmp1[:, i, :],
                            op0=ALU.mult, op1=ALU.add), reads=[asbn, 'cf', 'tmp1', 'od'], writes=['od'])
                        P.op('act', lambda e, i=i: e.activation(junk[:, :], od[:, i, :], AF.Square, accum_out=ss[:, i:i + 1]),
                             reads=['od', 'junk', 'ss'], writes=['junk', 'ss'])
                    P.op('dve', lambda e: e.tensor_scalar(ss[:, 4:8], ss[:, 0:4], 1.0 / 128, EPS, ALU.mult, ALU.add),
                         reads=['ss'], writes=['ss'])
                    P.op('act', lambda e: e.activation(ss[:, 8:12], ss[:, 4:8], AF.Sqrt), reads=['ss'], writes=['ss'])
                    P.op('dve', lambda e: e.reciprocal(ss[:, 12:16], ss[:, 8:12]), reads=['ss'], writes=['ss'])
                    P.op('dve', lambda e: e.tensor_scalar(ss[:, 12:16], ss[:, 12:16], 0.8, None, ALU.mult), reads=['ss'], writes=['ss'])
                    for i in range(4):
                        P.op('dve', lambda e, i=i, hd=hd: e.scalar_tensor_tensor(
                            out=yd[:, i, hd * 128:(hd + 1) * 128], in0=od[:, i, :], scalar=ss[:, 12 + i:13 + i], in1=subg[:, :],
                            op0=ALU.mult, op1=ALU.mult), reads=['od', 'ss', 'subg', 'yd'], writes=['yd'])
                P.op('pool', lambda e: e.tensor_copy(ydb[:], yd[:]), reads=['yd', 'ydb'], writes=['ydb'])
                for c in range(4):
                    for i in range(4):
                        tr(pm[:, i * 128:(i + 1) * 128], ydb[:, i, c * 128:(c + 1) * 128], ['ydb'], ['pm'])
                    ys, ysn = ystg.next()
                    P.op('act', lambda e, ys=ys: e.copy(ys[:, :], pm[:, 0:512]), reads=['pm'], writes=[ysn])
                    P.dma('sp', lambda e, ys=ys, c=c, q0=q0: e.dma_start(out=YT[4 + c, :, q0:q0 + 512], in_=ys[:, :]),
                          reads=[ysn], writes=['YT'])
            P.barrier()
            P.emit()
        if DEBUG == 'p2b':
            return nc, 1

        def layernorm(es_, pfx, tiles, r, rn, gbc, bbc, o, on):
            st, mv, t1 = tiles
            P.op('dve', lambda e: e.bn_stats(st[:, 0:6], r[:, 0:512]), reads=[rn, pfx + 'st'], writes=[pfx + 'st'])
            P.op('dve', lambda e: e.bn_stats(st[:, 6:12], r[:, 512:1024]), reads=[rn, pfx + 'st'], writes=[pfx + 'st'])
            P.op('dve', lambda e: e.bn_aggr(mv[:, 0:2], st[:, 0:12]), reads=[pfx + 'st', pfx + 'mv'], writes=[pfx + 'mv'])
            P.op('dve', lambda e: e.tensor_scalar(mv[:, 2:3], mv[:, 1:2], 1.0, EPS, ALU.mult, ALU.add), reads=[pfx + 'mv'], writes=[pfx + 'mv'])
            P.op('act', lambda e: e.activation(mv[:, 3:4], mv[:, 2:3], AF.Sqrt), reads=[pfx + 'mv'], writes=[pfx + 'mv'])
            P.op('dve', lambda e: e.reciprocal(mv[:, 4:5], mv[:, 3:4]), reads=[pfx + 'mv'], writes=[pfx + 'mv'])
            P.op('dve', lambda e: e.tensor_scalar(t1[:, :], r[:, :], mv[:, 0:1], mv[:, 4:5], ALU.subtract, ALU.mult),
                 reads=[rn, pfx + 'mv', pfx + 't1'], writes=[pfx + 't1'])
            P.op('pool', lambda e: e.tensor_tensor(t1[:, :], t1[:, :], gbc[:, :], ALU.mult), reads=[pfx + 't1', pfx + 'g'], writes=[pfx + 't1'])
            P.op('pool', lambda e: e.tensor_tensor(o[:, :], t1[:, :], bbc[:, :], ALU.add), reads=[pfx + 't1', pfx + 'b', on], writes=[on])

        with contextlib.ExitStack() as es:
            wbn = SB(es, 'wbn', [128, 4, D], BF16)
            wbd = SB(es, 'wbd', [128, 4, D], BF16)
            wo = SB(es, 'wo', [128, 8, D], BF16)
            wst = Rot('wst3', [SB(es, 'wst3_%d' % i, [128, 1024], F32) for i in range(3)])
            g1 = SB(es, 'g1', [128, D], F32); b1 = SB(es, 'b1', [128, D], F32)
            ynT = Rot('ynT', [SB(es, 'ynT%d' % i, [128, 4, 512], BF16) for i in range(2)])
            ydT = Rot('ydT', [SB(es, 'ydT%d' % i, [128, 4, 512], BF16) for i in range(2)])
            gnr = Rot('gn', [SB(es, 'gn%d' % i, [128, 8, 512], BF16) for i in range(2)])
            gdr = Rot('gd', [SB(es, 'gd%d' % i, [128, 8, 512], BF16) for i in range(2)])
            mrg = Rot('mrg', [SB(es, 'mrg%d' % i, [128, 8, 512], BF16) for i in range(2)])
            ta = Rot('ta', [SB(es, 'ta%d' % i, [128, 512], F32) for i in range(2)])
            tb_ = Rot('tb', [SB(es, 'tb%d' % i, [128, 512], F32) for i in range(2)])
            xt = Rot('xt3', [SB(es, 'xt3_%d' % i, [128, D], F32) for i in range(2)])
            x1t = Rot('x1t', [SB(es, 'x1t%d' % i, [128, D], F32) for i in range(2)])
            x1b = Rot('x1b', [SB(es, 'x1b%d' % i, [128, D], BF16) for i in range(2)])
            x1s = Rot('x1s', [SB(es, 'x1s%d' % i, [128, 8, 128], BF16) for i in range(2)])
            lnt = (SB(es, 'l1st', [128, 12], F32), SB(es, 'l1mv', [128, 8], F32), SB(es, 'l1t1', [128, D], F32))
            pa3 = Rot('pa3', [PSB(es, 'pa3_%d' % i) for i in range(2)])
            pb3 = Rot('pb3', [PSB(es, 'pb3_%d' % i) for i in range(2)])
            pz3 = Rot('pz3', [PSB(es, 'pz3_%d' % i) for i in range(2)])
            pt3 = PSB(es, 'pt3', BF16)
            for kc in range(4):
                load_cast(es, lambda c0, c1, kc=kc: wbn[:, kc, c0:c1], lambda c0, c1, kc=kc: w_bn[kc * 128:(kc + 1) * 128, c0:c1],
                          128, D, [(0, 1024)], 'wbn', wst)
                load_cast(es, lambda c0, c1, kc=kc: wbd[:, kc, c0:c1], lambda c0, c1, kc=kc: w_bd[kc * 128:(kc + 1) * 128, c0:c1],
                          128, D, [(0, 1024)], 'wbd', wst)
            for kc in range(8):
                load_cast(es, lambda c0, c1, kc=kc: wo[:, kc, c0:c1], lambda c0, c1, kc=kc: w_out[kc * 128:(kc + 1) * 128, c0:c1],
                          128, D, [(0, 1024)], 'wo', wst)
            P.dma('sp', lambda e: e.dma_start(out=g1[:], in_=ln1g.partition_broadcast(128)), writes=['l1g'])
            P.dma('sp', lambda e: e.dma_start(out=b1[:], in_=ln1b.partition_broadcast(128)), writes=['l1b'])
            for tb in range(QB0, NQB):
                t0 = tb * 512
                yn, ynn = ynT.next(); ydt, ydn = ydT.next(); gn, gnn = gnr.next(); gd, gdn = gdr.next()
                P.dma('sp', lambda e, yn=yn, t0=t0: e.dma_start(out=yn[:], in_=YT[0:4, :, t0:t0 + 512].rearrange("c p t -> p c t")),
                      reads=['YT'], writes=[ynn])
                P.dma('sp', lambda e, ydt=ydt, t0=t0: e.dma_start(out=ydt[:], in_=YT[4:8, :, t0:t0 + 512].rearrange("c p t -> p c t")),
                      reads=['YT'], writes=[ydn])
                P.dma('sp', lambda e, gn=gn, t0=t0: e.dma_start(out=gn[:], in_=FM[16:24, :, t0:t0 + 512].rearrange("c p t -> p c t")),
                      reads=FMALL, writes=[gnn])
                P.dma('sp', lambda e, gd=gd, t0=t0: e.dma_start(out=gd[:], in_=FM[24:32, :, t0:t0 + 512].rearrange("c p t -> p c t")),
                      reads=FMALL, writes=[gdn])
                mg, mgn = mrg.next()
                for fc in range(8):
                    pa, pan = pa3.next(); pb_, pbn = pb3.next()
                    for kc in range(4):
                        mm(pa[:, :], wbn[:, kc, fc * 128:(fc + 1) * 128], yn[:, kc, :], kc == 0, kc == 3, ['wbn', ynn], [pan])
                    for kc in range(4):
                        mm(pb_[:, :], wbd[:, kc, fc * 128:(fc + 1) * 128], ydt[:, kc, :], kc == 0, kc == 3, ['wbd', ydn], [pbn])
                    t1, t1n = ta.next(); t2, t2n = tb_.next()
                    P.op('dve', lambda e, t1=t1, pa=pa, gn=gn, fc=fc: e.tensor_tensor(t1[:, :], pa[:, :], gn[:, fc, :], ALU.mult),
                         reads=[pan, gnn], writes=[t1n])
                    P.op('dve', lambda e, t2=t2, pb_=pb_, gd=gd, fc=fc: e.tensor_tensor(t2[:, :], pb_[:, :], gd[:, fc, :], ALU.mult),
                         reads=[pbn, gdn], writes=[t2n])
                    P.op('pool', lambda e, mg=mg, t1=t1, t2=t2, fc=fc: e.tensor_tensor(mg[:, fc, :], t1[:, :], t2[:, :], ALU.add),
                         reads=[t1n, t2n, mgn], writes=[mgn])
                for i in range(4):
                    tok0 = t0 + i * 128
                    xx, xxn = xt.next(); o1, o1n = x1t.next(); ob, obn = x1b.next(); xs_, xsn = x1s.next()
                    r_, rn = xx, xxn
                    P.dma('sp', lambda e, xx=xx, tok0=tok0: e.dma_start(out=xx[:], in_=x[tok0:tok0 + 128, :]), writes=[xxn])
                    for half in range(2):
                        pz, pzn = pz3.next()
                        for fc in range(8):
                            mm(pz[:, :], mg[:, fc, i * 128:(i + 1) * 128], wo[:, fc, half * 512:(half + 1) * 512], fc == 0, fc == 7,
                               [mgn, 'wo'], [pzn])
                        P.op('dve', lambda e, r_=r_, xx=xx, pz=pz, half=half: e.scalar_tensor_tensor(
                            out=r_[:, half * 512:(half + 1) * 512], in0=xx[:, half * 512:(half + 1) * 512], scalar=ALPHA, in1=pz[:, :],
                            op0=ALU.mult, op1=ALU.add), reads=[xxn, pzn], writes=[rn])
                    layernorm(es, 'l1', lnt, r_, rn, g1, b1, o1, o1n)
                    P.dma('sp', lambda e, o1=o1, tok0=tok0: e.dma_start(out=X1[tok0:tok0 + 128, :], in_=o1[:]), reads=[o1n], writes=['X1'])
                    P.op('act', lambda e, ob=ob, o1=o1: e.copy(ob[:], o1[:]), reads=[o1n, obn], writes=[obn])
                    for kc in range(8):
                        tr(pt3[:, kc * 128:(kc + 1) * 128], ob[:, kc * 128:(kc + 1) * 128], [obn], ['pt3'])
                    P.op('act', lambda e, xs_=xs_: e.copy(xs_[:], pt3[:].rearrange("p (a b) -> p a b", b=128)), reads=['pt3', xsn], writes=[xsn])
                    P.dma('sp', lambda e, xs_=xs_, tok0=tok0: e.dma_start(
                        out=X1T[:, :, tok0:tok0 + 128].rearrange("c p t -> p c t"), in_=xs_[:]), reads=[xsn], writes=['X1T'])
            P.barrier()
            P.emit()
        if DEBUG == 'p3':
            return nc, 1

        with contextlib.ExitStack() as es:
            x1T = SB(es, 'x1T', [128, 8, S], BF16)
            cw = SB(es, 'cw', [128, NFC * 3], F32); cb_ = SB(es, 'cbv', [128, NFC], F32)
            wgs = Rot('wgs', [SB(es, 'wgs%d' % i, [128, 8, 128], F32) for i in range(2)])
            wus = Rot('wus', [SB(es, 'wus%d' % i, [128, 8, 128], F32) for i in range(2)])
            wgb = Rot('wgb', [SB(es, 'wgb%d' % i, [128, 8, 128], BF16) for i in range(2)])
            wub = Rot('wub', [SB(es, 'wub%d' % i, [128, 8, 128], BF16) for i in range(2)])
            gext = Rot('gext', [SB(es, 'gext%d' % i, [128, 514], F32) for i in range(2)])
            cv_ = Rot('cv', [SB(es, 'cv%d' % i, [128, 512], F32) for i in range(2)])
            ga = Rot('ga', [SB(es, 'ga%d' % i, [128, 512], F32) for i in range(2)])
            ast = Rot('ast', [SB(es, 'ast%d' % i, [128, 512], BF16) for i in range(3)])
            pg = Rot('pg', [PSB(es, 'pg%d' % i) for i in range(3)])
            pu = Rot('pu', [PSB(es, 'pu%d' % i) for i in range(3)])
            for kc in range(8):
                P.dma('sp', lambda e, kc=kc: e.dma_start(out=x1T[:, kc, QB0 * 512:S], in_=X1T[kc, :, QB0 * 512:S]), reads=['X1T'], writes=['x1T'])
            hvs = SB(es, 'hvs', [128, 1], F32)
            P.dma('sp', lambda e: e.dma_start(out=hvs[:], in_=hvin[:, :]), writes=['hvs'])
            P.dma('sp', lambda e: e.dma_start(out=cw[:], in_=convw[:, :]), writes=['cw'])
            P.dma('sp', lambda e: e.dma_start(out=cb_[:], in_=convb[:, :]), writes=['cbv'])
            for j in range(NFC):
                wg_s, wgsn = wgs.next(); wu_s, wusn = wus.next(); wg_b, wgbn = wgb.next(); wu_b, wubn = wub.next()
                P.dma('sp', lambda e, wg_s=wg_s, j=j: e.dma_start(
                    out=wg_s[:], in_=w_ffn[:, j * 128:(j + 1) * 128].rearrange("(kc p) n -> p kc n", p=128)), writes=[wgsn])
                P.dma('sp', lambda e, wu_s=wu_s, j=j: e.dma_start(
                    out=wu_s[:], in_=w_ffn[:, DFF + j * 128:DFF + (j + 1) * 128].rearrange("(kc p) n -> p kc n", p=128)), writes=[wusn])
                P.op('pool', lambda e, wg_b=wg_b, wg_s=wg_s: e.tensor_copy(wg_b[:], wg_s[:]), reads=[wgsn, wgbn], writes=[wgbn])
                P.op('pool', lambda e, wu_b=wu_b, wu_s=wu_s: e.tensor_copy(wu_b[:], wu_s[:]), reads=[wusn, wubn], writes=[wubn])
                prev = None
                ph_, phn = pg.next()
                for kc in range(8):
                    mm(ph_[:, 0:2], wg_b[:, kc, :], x1T[:, kc, OWN0 - 2:OWN0], kc == 0, kc == 7, [wgbn, 'x1T'], [phn])
                for tb in range(QB0 + 1, NQB):
                    t0 = tb * 512
                    pg_, pgn = pg.next(); pu_, pun = pu.next()
                    for kc in range(8):
                        mm(pg_[:, :], wg_b[:, kc, :], x1T[:, kc, t0:t0 + 512], kc == 0, kc == 7, [wgbn, 'x1T'], [pgn])
                    for kc in range(8):
                        mm(pu_[:, :], wu_b[:, kc, :], x1T[:, kc, t0:t0 + 512], kc == 0, kc == 7, [wubn, 'x1T'], [pun])
                    ge, gen = gext.next(); cvt, cvn = cv_.next(); gat, gan = ga.next(); at, atn = ast.next()
                    P.op('act', lambda e, ge=ge, pg_=pg_: e.copy(ge[:, 2:514], pg_[:, :]), reads=[pgn, gen], writes=[gen])
                    if prev is None:
                        P.op('dve', lambda e, ge=ge, ph_=ph_: e.tensor_scalar(ge[:, 0:2], ph_[:, 0:2], hvs[:, 0:1], None, ALU.mult),
                             reads=[gen, phn, 'hvs'], writes=[gen])
                    else:
                        P.op('pool', lambda e, ge=ge, pv=prev[0]: e.tensor_copy(ge[:, 0:2], pv[:, 512:514]), reads=[gen, prev[1]], writes=[gen])
                    prev = (ge, gen)
                    P.op('dve', lambda e, cvt=cvt, ge=ge, j=j: e.tensor_scalar(
                        cvt[:, :], ge[:, 2:514], cw[:, 3 * j + 2:3 * j + 3], cb_[:, j:j + 1], ALU.mult, ALU.add),
                        reads=[gen, 'cw', 'cbv', cvn], writes=[cvn])
                    P.op('dve', lambda e, cvt=cvt, ge=ge, j=j: e.scalar_tensor_tensor(
                        out=cvt[:, :], in0=ge[:, 1:513], scalar=cw[:, 3 * j + 1:3 * j + 2], in1=cvt[:, :], op0=ALU.mult, op1=ALU.add),
                        reads=[gen, 'cw', cvn], writes=[cvn])
                    P.op('dve', lambda e, cvt=cvt, ge=ge, j=j: e.scalar_tensor_tensor(
                        out=cvt[:, :], in0=ge[:, 0:512], scalar=cw[:, 3 * j:3 * j + 1], in1=cvt[:, :], op0=ALU.mult, op1=ALU.add),
                        reads=[gen, 'cw', cvn], writes=[cvn])
                    P.op('act', lambda e, gat=gat, cvt=cvt: e.activation(gat[:, :], cvt[:, :], AF.Gelu_apprx_tanh), reads=[cvn, gan], writes=[gan])
                    P.op('dve', lambda e, at=at, gat=gat, pu_=pu_: e.tensor_tensor(at[:, :], gat[:, :], pu_[:, :], ALU.mult),
                         reads=[gan, pun, atn], writes=[atn])
                    P.dma('sp', lambda e, at=at, j=j, t0=t0: e.dma_start(out=AT[j, :, t0:t0 + 512], in_=at[:, :]), reads=[atn], writes=['AT'])
            P.barrier()
            P.emit()
        if DEBUG == 'p4a':
            return nc, 1

        with contextlib.ExitStack() as es:
            wdn = SB(es, 'wdn', [128, NFC, D], BF16)
            wpg = SB(es, 'wpg', [128, 8, D], BF16)
            wpp = SB(es, 'wpp', [128, 2, D], BF16)
            wst = Rot('wst4', [SB(es, 'wst4_%d' % i, [128, 1024], F32) for i in range(3)])
            g2 = SB(es, 'g2', [128, D], F32); b2 = SB(es, 'b2', [128, D], F32)
            aTr = Rot('aT', [SB(es, 'aT%d' % i, [128, NFC, 512], BF16) for i in range(2)])
            xt = Rot('xt4', [SB(es, 'xt4_%d' % i, [128, D], F32) for i in range(2)])
            x2t = Rot('x2t', [SB(es, 'x2t%d' % i, [128, D], F32) for i in range(2)])
            x2b = Rot('x2b', [SB(es, 'x2b%d' % i, [128, D], BF16) for i in range(2)])
            x2s = Rot('x2s', [SB(es, 'x2s%d' % i, [128, 8, 128], BF16) for i in range(2)])
            sg = Rot('sg', [SB(es, 'sg%d' % i, [128, D], F32) for i in range(2)])
            pf = Rot('pf', [SB(es, 'pf%d' % i, [128, 256], F32) for i in range(2)])
            pbf = Rot('pbf', [SB(es, 'pbf%d' % i, [128, 256], BF16) for i in range(2)])
            pTs = Rot('pTs', [SB(es, 'pTs%d' % i, [128, 2, 128], BF16) for i in range(2)])
            lnt = (SB(es, 'l2st', [128, 12], F32), SB(es, 'l2mv', [128, 8], F32), SB(es, 'l2t1', [128, D], F32))
            pz = Rot('pz4', [PSB(es, 'pz4_%d' % i) for i in range(2)])
            pgt = Rot('pgt', [PSB(es, 'pgt%d' % i) for i in range(2)])
            ppp = Rot('ppp', [PSB(es, 'ppp%d' % i) for i in range(2)])
            pt4 = PSB(es, 'pt4', BF16)
            pt5 = PSB(es, 'pt5', BF16)
            for j in range(NFC):
                load_cast(es, lambda c0, c1, j=j: wdn[:, j, c0:c1], lambda c0, c1, j=j: w_down[j * 128:(j + 1) * 128, c0:c1],
                          128, D, [(0, 1024)], 'wdn', wst)
            for kc in range(8):
                load_cast(es, lambda c0, c1, kc=kc: wpg[:, kc, c0:c1], lambda c0, c1, kc=kc: w_pg[kc * 128:(kc + 1) * 128, c0:c1],
                          128, D, [(0, 1024)], 'wpg', wst)
            for kc in range(2):
                load_cast(es, lambda c0, c1, kc=kc: wpp[:, kc, c0:c1], lambda c0, c1, kc=kc: w_pp[kc * 128:(kc + 1) * 128, c0:c1],
                          128, D, [(0, 1024)], 'wpp', wst)
            P.dma('sp', lambda e: e.dma_start(out=g2[:], in_=ln2g.partition_broadcast(128)), writes=['l2g'])
            P.dma('sp', lambda e: e.dma_start(out=b2[:], in_=ln2b.partition_broadcast(128)), writes=['l2b'])
            for tb in range(QB0 + 1, NQB):
                t0 = tb * 512
                aT, aTn = aTr.next()
                for hfj in range(2):
                    P.dma('sp', lambda e, aT=aT, t0=t0, hfj=hfj: e.dma_start(
                        out=aT[:, hfj * 11:(hfj + 1) * 11, :], in_=AT[hfj * 11:(hfj + 1) * 11, :, t0:t0 + 512].rearrange("c p t -> p c t")),
                        reads=['AT'], writes=[aTn])
                for i in range(4):
                    tok0 = t0 + i * 128
                    xx, xxn = xt.next(); o2, o2n = x2t.next(); ob, obn = x2b.next(); xs_, xsn = x2s.next()
                    r_, rn = xx, xxn
                    sgt, sgn = sg.next(); pft, pfn = pf.next(); pbt, pbn_ = pbf.next(); pTt, pTn = pTs.next()
                    oo, oon = sgt, sgn
                    P.dma('sp', lambda e, xx=xx, tok0=tok0: e.dma_start(out=xx[:], in_=X1[tok0:tok0 + 128, :]), reads=['X1'], writes=[xxn])
                    P.dma('sp', lambda e, pft=pft, tok0=tok0: e.dma_start(out=pft[:], in_=pin[tok0 - OWN0:tok0 - OWN0 + 128, :]), writes=[pfn])
                    for half in range(2):
                        pz_, pzn = pz.next()
                        for j in range(NFC):
                            mm(pz_[:, :], aT[:, j, i * 128:(i + 1) * 128], wdn[:, j, half * 512:(half + 1) * 512], j == 0, j == NFC - 1,
                               [aTn, 'wdn'], [pzn])
                        P.op('dve', lambda e, r_=r_, xx=xx, pz_=pz_, half=half: e.scalar_tensor_tensor(
                            out=r_[:, half * 512:(half + 1) * 512], in0=xx[:, half * 512:(half + 1) * 512], scalar=ALPHA, in1=pz_[:, :],
                            op0=ALU.mult, op1=ALU.add), reads=[xxn, pzn], writes=[rn])
                    layernorm(es, 'l2', lnt, r_, rn, g2, b2, o2, o2n)
                    P.op('act', lambda e, ob=ob, o2=o2: e.copy(ob[:], o2[:]), reads=[o2n, obn], writes=[obn])
                    for kc in range(8):
                        tr(pt4[:, kc * 128:(kc + 1) * 128], ob[:, kc * 128:(kc + 1) * 128], [obn], ['pt4'])
                    P.op('act', lambda e, xs_=xs_: e.copy(xs_[:], pt4[:].rearrange("p (a b) -> p a b", b=128)), reads=['pt4', xsn], writes=[xsn])
                    P.op('pool', lambda e, pbt=pbt, pft=pft: e.tensor_copy(pbt[:], pft[:]), reads=[pfn, pbn_], writes=[pbn_])
                    for kc in range(2):
                        tr(pt5[:, kc * 128:(kc + 1) * 128], pbt[:, kc * 128:(kc + 1) * 128], [pbn_], ['pt5'])
                    P.op('act', lambda e, pTt=pTt: e.copy(pTt[:], pt5[:, 0:256].rearrange("p (a b) -> p a b", b=128)),
                         reads=['pt5', pTn], writes=[pTn])
                    for half in range(2):
                        pg_, pgn = pgt.next(); pp_, ppn = ppp.next()
                        for kc in range(8):
                            mm(pg_[:, :], xs_[:, kc, :], wpg[:, kc, half * 512:(half + 1) * 512], kc == 0, kc == 7, [xsn, 'wpg'], [pgn])
                        for kc in range(2):
                            mm(pp_[:, :], pTt[:, kc, :], wpp[:, kc, half * 512:(half + 1) * 512], kc == 0, kc == 1, [pTn, 'wpp'], [ppn])
                        P.op('act', lambda e, sgt=sgt, pg_=pg_, half=half: e.activation(
                            sgt[:, half * 512:(half + 1) * 512], pg_[:, :], AF.Sigmoid), reads=[pgn, sgn], writes=[sgn])
                        P.op('dve', lambda e, sgt=sgt, pp_=pp_, half=half: e.tensor_tensor(
                            sgt[:, half * 512:(half + 1) * 512], sgt[:, half * 512:(half + 1) * 512], pp_[:, :], ALU.mult),
                            reads=[ppn, sgn], writes=[sgn])
                    P.op('pool', lambda e, oo=oo, sgt=sgt, o2=o2: e.tensor_tensor(oo[:, :], sgt[:, :], o2[:, :], ALU.add),
                         reads=[sgn, o2n], writes=[oon])
                    P.dma('sp', lambda e, oo=oo, tok0=tok0: e.dma_start(out=out[tok0 - OWN0:tok0 - OWN0 + 128, :], in_=oo[:]), reads=[oon], writes=['out'])
            P.barrier()
            P.emit()
    return nc, None
```

```python
import math
import contextlib
import numpy as np
import concourse.bass as bass
import concourse.mybir as mybir
from concourse.bass_utils import run_bass_kernel_spmd

F32 = mybir.dt.float32
BF16 = mybir.dt.bfloat16
AF = mybir.ActivationFunctionType
ALU = mybir.AluOpType
AX = mybir.AxisListType

S = 4096
D = 1024
NT = S // 128
NQB = S // 512
QB0 = 3
OWN0 = 2048
NOWN = 2048
DFF = 2816
NFC = DFF // 128
LFV = 4608
OFFD = 2064
NEG = -30000.0
ALPHA = 2.0 ** 0.25
EPS = 1e-5
DEBUG = None
DEBUG_OUT = ()


class Prog:
    ENG = ('pe', 'act', 'dve', 'pool', 'sp')
    KDMA = 14
    CAP = 1000

    def __init__(self, nc, es):
        self.nc = nc
        self.es = es
        self.nsem = 0
        self.q = {e: [] for e in self.ENG}
        self.sem = {e: self._newsem() for e in ('pe', 'act', 'dve', 'pool')}
        self.cnt = {e: 0 for e in ('pe', 'act', 'dve', 'pool')}
        self.last = {}
        self.dslot = {e: [[self._newsem(), 0] for _ in range(self.KDMA)] for e in ('sp', 'pool', 'act')}
        self.dlast = {e: [None] * self.KDMA for e in ('sp', 'pool', 'act')}
        self.dcnt = {e: 0 for e in ('sp', 'pool', 'act')}
        self.waited = {}
        self.res = {}
        self.nops = 0

    def _newsem(self):
        self.nsem += 1
        return self.es.enter_context(self.nc.semaphore('sm%d' % self.nsem))

    def _deps(self, reads, writes):
        deps = {}

        def add(tok):
            if tok is None:
                return
            s, v, e = tok
            if id(s) not in deps or deps[id(s)][1] < v:
                deps[id(s)] = (s, v, e)
        for r in reads:
            st = self.res.get(r)
            if st:
                add(st['w'])
        for r in writes:
            st = self.res.get(r)
            if st:
                add(st['w'])
                for t in st['r'].values():
                    add(t)
        return deps

    def _commit(self, tok, reads, writes):
        for r in reads:
            st = self.res.setdefault(r, {'w': None, 'r': {}})
            st['r'][id(tok[0])] = tok
        for r in writes:
            self.res[r] = {'w': tok, 'r': {}}

    def _waits(self, eng, deps, skip_eng=None):
        waits = []
        for _, (s, v, e) in deps.items():
            if skip_eng is not None and e == skip_eng:
                continue
            key = (eng, id(s))
            if self.waited.get(key, 0) >= v:
                continue
            self.waited[key] = v
            waits.append((s, v))
        return waits

    def op(self, eng, fn, reads=(), writes=()):
        self.nops += 1
        deps = self._deps(reads, writes)
        waits = self._waits(eng, deps, skip_eng=('pe' if eng == 'pe' else None))
        if self.cnt[eng] >= self.CAP:
            self.sem[eng] = self._newsem()
            self.cnt[eng] = 0
        self.cnt[eng] += 1
        tok = (self.sem[eng], self.cnt[eng], eng)
        self.last[eng] = tok
        self.q[eng].append((waits, fn, tok[0], 1))
        self._commit(tok, reads, writes)
        return tok

    def dma(self, eng, fn, reads=(), writes=()):
        self.nops += 1
        deps = self._deps(reads, writes)
        n = self.dcnt[eng]
        self.dcnt[eng] += 1
        k = n % self.KDMA
        prev = self.dlast[eng][k]
        if prev is not None:
            cur = deps.get(id(prev[0]))
            if cur is None or cur[1] < prev[1]:
                deps[id(prev[0])] = prev
        slot = self.dslot[eng][k]
        if slot[1] + 16 > self.CAP:
            slot[0] = self._newsem()
            slot[1] = 0
        slot[1] += 16
        tok = (slot[0], slot[1], 'dma_' + eng)
        self.dlast[eng][k] = tok
        waits = self._waits(eng, deps)
        self.q[eng].append((waits, fn, tok[0], 16))
        self._commit(tok, reads, writes)
        return tok

    def barrier(self):
        toks = [t for t in self.last.values()]
        for e in ('sp', 'pool', 'act'):
            toks += [t for t in self.dlast[e] if t is not None]
        for eng in self.ENG:
            deps = {id(t[0]): t for t in toks}
            waits = self._waits(eng, deps, skip_eng=(eng if eng in self.sem else None))
            if waits:
                self.q[eng].append((waits, None, None, 0))

    def emit(self):
        nc = self.nc
        q = self.q
        with nc.Block() as block:
            def runner(name):
                def f(e):
                    for waits, fn, s, inc in q[name]:
                        for ws, wv in waits:
                            e.wait_ge(ws, wv)
                        if fn is not None:
                            fn(e).then_inc(s, inc)
                return f
            block.tensor(runner('pe'))
            block.scalar(runner('act'))
            block.vector(runner('dve'))
            block.gpsimd(runner('pool'))
            block.sync(runner('sp'))
        self.q = {e: [] for e in self.ENG}


class Rot:
    def __init__(self, name, tiles):
        self.name = name
        self.tiles = tiles
        self.i = 0

    def next(self):
        k = self.i % len(self.tiles)
        self.i += 1
        return self.tiles[k], '%s%d' % (self.name, k)


def rel_bucket_np(n):
    n = np.maximum(n, 0)
    large = 16 + (np.log(np.maximum(n, 1).astype(np.float32) / np.float32(16)) / np.float32(math.log(128 / 16))
                  * np.float32(16)).astype(np.int32)
    large = np.minimum(large, 31)
    return np.where(n < 16, n, large)


def cmp_tile_type(q0, nq, nt):
    lo = q0 - (16 * (128 * nt + 127) + 31)
    hi = q0 + nq - 1 - (16 * (128 * nt) + 31)
    if hi < 0:
        return 'skip'
    if lo >= 113 and nt == 0:
        return 'zero'
    return 'bias'


def host_consts(hf=1):
    c = {}
    voff = 0 if hf == 1 else 2048
    a = np.arange(LFV)
    dist = a - OFFD
    oh = np.zeros((33, LFV), np.float32)
    b = rel_bucket_np(dist)
    ok = dist >= 0
    oh[b[ok], a[ok]] = 1.0
    oh[31, a[ok]] -= 1.0
    oh[32, a[~ok]] = 1.0
    c['ohd'] = oh
    c['ident'] = np.eye(128, dtype=np.float32)
    ut = np.zeros((128, 6, 4, 128), np.float32)
    for m in range(5):
        for i in range(4):
            if m - 1 - i > 0:
                ut[:, m, i, :] = NEG
    k = np.arange(128)[:, None]
    q = np.arange(128)[None, :]
    w4 = np.where(k > q, 0.0, NEG).astype(np.float32)
    ut[:, 5, 3, :] = w4
    c['utmpl'] = ut.reshape(128, 6 * 512)
    wc = np.zeros((128, 3, 4, 128), np.float32)
    for jj in range(3):
        for i in range(4):
            r = jj - 4 - i
            if r == -4:
                wc[:, jj, i, :] = w4
            elif r < -4:
                wc[:, jj, i, :] = NEG
    c['wctmpl'] = wc.reshape(128, 3 * 512)
    e = np.zeros((64, S), np.float32)
    e[np.arange(S) // 64, np.arange(S)] = 1.0
    c['eall'] = e
    n_cmp, n_slc = 255, 64
    jj, mm, nn = np.meshgrid(np.arange(n_slc), np.arange(4), np.arange(2), indexing='ij')
    ii = 4 * jj + mm - nn
    okk = (ii >= 0) & (ii < n_cmp)
    mat = np.zeros((256, n_slc), np.float32)
    np.add.at(mat, (ii[okk], jj[okk]), 1.0)
    cv = np.ones((256, 1), np.float32)
    cv[255] = 0.0
    cv[:voff // 16] = 0.0
    mat = mat * cv
    c['mslc'] = mat
    c['cvalid'] = np.ascontiguousarray(cv.reshape(2, 128).T)
    tv = np.arange(S)
    cur = (tv - voff) // 64
    j = np.arange(64)[None, :] - voff // 64
    forced = (j == 0) | ((cur[:, None] - j >= 0) & (cur[:, None] - j < 2))
    valid = (j <= cur[:, None]) & (j >= 0)
    am = np.where(valid & forced, 1e4, np.where(valid, 0.0, -1e4)).astype(np.float32)
    am[tv < voff] = 0.0
    vc_ = (np.arange(NT) * 128 >= voff).astype(np.float32)
    c['validc'] = np.ascontiguousarray(np.broadcast_to(vc_[None, :], (128, NT)))
    c['hv'] = np.full((128, 1), float(hf), np.float32)
    c['addmask'] = np.ascontiguousarray(am.reshape(NT, 128, 64).transpose(1, 0, 2)).reshape(128, NT * 64)
    return c


def prep_inputs(inputs, b, hf, consts):
    f = lambda a: np.ascontiguousarray(np.asarray(a, dtype=np.float32))
    w_in = f(inputs['w_in'])[0]
    offs = np.cumsum([0, 512, 128, 128, 128, 128, 128, 128, 24, 512, 512, 512, 1024, 1024])
    seg = lambda i: w_in[:, offs[i]:offs[i + 1]]
    nq = seg(0).reshape(D, 2, 4, 64)
    qn = [np.concatenate([nq[:, 0, r, :], nq[:, 1, r, :]], axis=1) for r in range(4)]
    w_fm = np.concatenate([seg(3), seg(5), seg(1), seg(2), seg(9)] + qn + [seg(8), seg(11), seg(12)], axis=1)
    w_tm = np.concatenate([seg(4), seg(6), seg(10), seg(7)], axis=1)
    m = dict(consts)
    tab = f(inputs['rel_bias_table'])
    m['tabaug'] = np.concatenate([tab, np.full((1, 12), NEG, np.float32)], axis=0)
    xb_ = f(inputs['x'])[b]
    m['x'] = xb_ if hf == 1 else np.concatenate([np.zeros((2048, D), np.float32), xb_[:2048]], axis=0)
    m['p'] = np.ascontiguousarray(f(inputs['p'])[0, b, hf * 2048:(hf + 1) * 2048])
    m['w_fm'] = np.ascontiguousarray(w_fm)
    m['w_tm'] = np.ascontiguousarray(w_tm)
    m['w1k'] = f(inputs['nsa_cmp_w1_k'])[0]; m['w1v'] = f(inputs['nsa_cmp_w1_v'])[0]
    m['w2k'] = f(inputs['nsa_cmp_w2_k'])[0]; m['w2v'] = f(inputs['nsa_cmp_w2_v'])[0]
    m['pekT'] = np.ascontiguousarray(f(inputs['nsa_cmp_pe_k'])[0].T)
    m['pevT'] = np.ascontiguousarray(f(inputs['nsa_cmp_pe_v'])[0].T)
    m['lam_in'] = np.concatenate([f(inputs['diff_lambda_q1']), f(inputs['diff_lambda_k1']),
                                  f(inputs['diff_lambda_q2']), f(inputs['diff_lambda_k2'])], axis=0).reshape(1, 256)
    m['subln'] = f(inputs['diff_subln_g']).reshape(1, 128)
    m['w_bn'] = f(inputs['w_branch_nsa'])[0]; m['w_bd'] = f(inputs['w_branch_diff'])[0]
    m['w_out'] = f(inputs['w_out'])[0]
    m['ln1g'] = f(inputs['ln1_g']).reshape(1, D); m['ln1b'] = f(inputs['ln1_b']).reshape(1, D)
    m['ln2g'] = f(inputs['ln2_g']).reshape(1, D); m['ln2b'] = f(inputs['ln2_b']).reshape(1, D)
    m['w_ffn'] = f(inputs['w_ffn_in'])[0]
    cw = f(inputs['ffn_conv_w'])[0]
    m['convw'] = np.ascontiguousarray(cw.reshape(3, NFC, 128).transpose(2, 1, 0)).reshape(128, NFC * 3)
    m['convb'] = np.ascontiguousarray(f(inputs['ffn_conv_b'])[0].reshape(NFC, 128).T)
    m['w_down'] = f(inputs['w_ffn_down'])[0]
    m['w_pp'] = f(inputs['w_ple_proj'])[0]; m['w_pg'] = f(inputs['w_ple_gate'])[0]
    return m


def kernel(**inputs):
    nc, dbg = build_nc()
    consts = [host_consts(0), host_consts(1)]
    in_maps = [prep_inputs(inputs, c // 2, c % 2, consts[c % 2]) for c in range(8)]
    res = run_bass_kernel_spmd(nc, in_maps, core_ids=list(range(8)))
    if dbg is not None:
        return res
    full = np.empty((4, S, D), np.float32)
    for c in range(8):
        full[c // 2, (c % 2) * 2048:(c % 2 + 1) * 2048] = np.asarray(res.results[c]['out'], dtype=np.float32)
    return full


def build_nc():
    nc = bass.Bass("TRN2", target_bir_lowering=False)

    def din(name, shape, dt=F32):
        return nc.dram_tensor(name, list(shape), dt, kind="ExternalInput").ap()

    def dscr(name, shape, dt=BF16):
        kind = "ExternalOutput" if (DEBUG and name in DEBUG_OUT) else "Internal"
        return nc.dram_tensor(name, list(shape), dt, kind=kind).ap()

    x = din('x', [S, D])
    pin = din('p', [NOWN, 256])
    validc = din('validc', [128, NT])
    hvin = din('hv', [128, 1])
    w_fm = din('w_fm', [D, 4096])
    w_tm = din('w_tm', [D, 792])
    ohd = din('ohd', [33, LFV])
    tabaug = din('tabaug', [33, 12])
    ident_d = din('ident', [128, 128])
    utmpl = din('utmpl', [128, 6 * 512])
    wctmpl = din('wctmpl', [128, 3 * 512])
    eall = din('eall', [64, S])
    mslc = din('mslc', [256, 64])
    cvalid = din('cvalid', [128, 2])
    addmask = din('addmask', [128, NT * 64])
    w1k = din('w1k', [2048, 256]); w1v = din('w1v', [2048, 256])
    w2k = din('w2k', [256, 64]); w2v = din('w2v', [256, 64])
    pekT = din('pekT', [64, 32]); pevT = din('pevT', [64, 32])
    lam_in = din('lam_in', [1, 256])
    subln = din('subln', [1, 128])
    w_bn = din('w_bn', [512, D]); w_bd = din('w_bd', [512, D]); w_out = din('w_out', [D, D])
    ln1g = din('ln1g', [1, D]); ln1b = din('ln1b', [1, D]); ln2g = din('ln2g', [1, D]); ln2b = din('ln2b', [1, D])
    w_ffn = din('w_ffn', [D, 2 * DFF])
    convw = din('convw', [128, NFC * 3])
    convb = din('convb', [128, NFC])
    w_down = din('w_down', [DFF, D])
    w_pp = din('w_pp', [256, D]); w_pg = din('w_pg', [D, D])
    out = nc.dram_tensor('out', [NOWN, D], F32, kind="ExternalOutput").ap()

    FM = dscr('FM', [32, 128, S])
    VTM = dscr('VTM', [S, 784])
    NGs = dscr('NG', [S, 24], F32)
    FVR = dscr('FVR', [12, 128, LFV])
    YT = dscr('YT', [8, 128, S])
    X1 = dscr('X1', [S, D], F32)
    X1T = dscr('X1T', [8, 128, S])
    AT = dscr('AT', [NFC, 128, S])
    DBG = dscr('DBG', [128, 4096], F32)

    def toep(h, base, pk, n):
        return bass.AP(FVR.tensor, h * 128 * LFV + OFFD + base, [[LFV - pk, 128], [1, n]])

    with contextlib.ExitStack() as esg:
        P = Prog(nc, esg)

        def SB(es, name, shape, dt):
            return es.enter_context(nc.sbuf_tensor(name, list(shape), dt))

        def PSB(es, name, dt=F32):
            return es.enter_context(nc.psum_tensor(name, [128, 512] if dt == F32 else [128, 1024], dt))

        def mm(o, l, r, st, sp, reads, writes):
            return P.op('pe', lambda e: e.matmul(o, l, r, start=st, stop=sp, skip_group_check=True), reads, writes)

        def tr(o, i_, reads, writes):
            return P.op('pe', lambda e: e.transpose(o, i_, identb[:]), list(reads) + ['identb'], writes)

        def load_cast(es_tmp, dst_fn, src_fn, nrows, ncols, pieces, wname, stg, defer=None):
            if defer is not None:
                for pc_ in pieces:
                    defer.append(lambda pc_=pc_: load_cast(es_tmp, dst_fn, src_fn, nrows, ncols, [pc_], wname, stg))
                return
            for pi, (c0, c1) in enumerate(pieces):
                ws_, wsn = stg.next()
                P.dma('sp', lambda e, ws_=ws_, c0=c0, c1=c1: e.dma_start(out=ws_[0:nrows, 0:c1 - c0], in_=src_fn(c0, c1)),
                      writes=[wsn])
                ce = 'pool' if pi % 2 == 0 else 'dve'
                P.op(ce, lambda e, ws_=ws_, c0=c0, c1=c1: e.tensor_copy(dst_fn(c0, c1), ws_[0:nrows, 0:c1 - c0]),
                     reads=[wsn], writes=[wname])

        identb = SB(esg, 'identb', [128, 128], BF16)
        zb = SB(esg, 'zb', [128, 512], BF16)
        KCT = SB(esg, 'KCT', [64, 2, 256], BF16)
        VCA = SB(esg, 'VCA', [128, 2, 2, 130], BF16)
        P.dma('pool', lambda e: e.dma_start(out=identb[:], in_=ident_d[:, :]), writes=['identb'])
        P.op('pool', lambda e: e.memset(zb[:], 0.0), writes=['zb'])

        es01 = contextlib.ExitStack()
        wf = SB(es01, 'wf', [128, 8, 4096], BF16)
        wt = SB(es01, 'wt', [128, 8, 792], BF16)
        wst = Rot('wst', [SB(es01, 'wst%d' % i, [128, 1024], F32) for i in range(3)])
        pre1 = []
        for kc in range(8):
            load_cast(es01, lambda c0, c1, kc=kc: wf[:, kc, c0:c1], lambda c0, c1, kc=kc: w_fm[kc * 128:(kc + 1) * 128, c0:c1],
                      128, 4096, [(i * 1024, (i + 1) * 1024) for i in range(4)], 'wf', wst, defer=pre1)
            load_cast(es01, lambda c0, c1, kc=kc: wt[:, kc, c0:c1], lambda c0, c1, kc=kc: w_tm[kc * 128:(kc + 1) * 128, c0:c1],
                      128, 792, [(0, 792)], 'wt', wst, defer=pre1)

        with contextlib.ExitStack() as es:
            ohf = SB(es, 'ohf', [33, LFV], F32)
            ohs = SB(es, 'ohs', [33, LFV], BF16)
            tabs = SB(es, 'tabs', [33, 12], F32)
            ones33 = SB(es, 'ones33', [33, 128], F32)
            tabrep = SB(es, 'tabrep', [33, 12, 128], BF16)
            fvs = Rot('fvs', [SB(es, 'fvs%d' % i, [128, LFV], BF16) for i in range(2)])
            pb = Rot('p0b', [PSB(es, 'p0b%d' % i) for i in range(4)])
            for cch in range(3):
                P.dma('sp', lambda e, cch=cch: e.dma_start(out=ohf[:, cch * 1536:(cch + 1) * 1536],
                                                           in_=ohd[:, cch * 1536:(cch + 1) * 1536]), writes=['ohf'])
            P.dma('sp', lambda e: e.dma_start(out=tabs[:], in_=tabaug[:, :]), writes=['tabs'])
            P.op('dve', lambda e: e.tensor_copy(ohs[:], ohf[:]), reads=['ohf'], writes=['ohs'])
            P.op('pool', lambda e: e.memset(ones33[:], 1.0), writes=['ones33'])
            for h in range(12):
                P.op('dve', lambda e, h=h: e.tensor_scalar(tabrep[:, h, :], ones33[:], tabs[:, h:h + 1], None, ALU.mult),
                     reads=['ones33', 'tabs'], writes=['tabrep'])
            for h in range(12):
                for _ in range(4):
                    if pre1:
                        pre1.pop(0)()
                fv, fvn = fvs.next()
                for cch in range(LFV // 512):
                    b_, bn = pb.next()
                    mm(b_[:, :], tabrep[:, h, :], ohs[:, cch * 512:(cch + 1) * 512], True, True, ['ohs', 'tabrep'], [bn])
                    if cch % 2 == 0:
                        P.op('dve', lambda e, b_=b_, fv=fv, cch=cch: e.tensor_copy(fv[:, cch * 512:(cch + 1) * 512], b_[:, :]),
                             reads=[bn], writes=[fvn])
                    else:
                        P.op('act', lambda e, b_=b_, fv=fv, cch=cch: e.copy(fv[:, cch * 512:(cch + 1) * 512], b_[:, :]),
                             reads=[bn], writes=[fvn])
                P.dma('sp', lambda e, fv=fv, h=h: e.dma_start(out=FVR[h], in_=fv[:]), reads=[fvn], writes=['FVR'])
            while pre1:
                pre1.pop(0)()
            P.barrier()
            P.emit()
        if DEBUG == 'p0':
            return nc, 1

        with contextlib.ExitStack() as es:
            xin = Rot('xin', [SB(es, 'xin%d' % i, [128, D], F32) for i in range(2)])
            xbf = Rot('xbf', [SB(es, 'xbf%d' % i, [128, D], BF16) for i in range(2)])
            xTr = Rot('xT', [SB(es, 'xT%d' % i, [128, 8, 512], BF16) for i in range(2)])
            stg = Rot('stg', [SB(es, 'stg%d' % i, [128, 512], BF16) for i in range(4)])
            tms = Rot('tms', [SB(es, 'tms%d' % i, [128, 784], BF16) for i in range(2)])
            ngs = Rot('ngs', [SB(es, 'ngs%d' % i, [128, 24], F32) for i in range(2)])
            pT = Rot('pT', [PSB(es, 'pT%d' % i, BF16) for i in range(2)])
            pacc = Rot('pacc', [PSB(es, 'pacc%d' % i) for i in range(5)])
            for t_ in tms.tiles:
                P.op('pool', lambda e, t_=t_: e.memset(t_[:], 1.0), writes=['tms0', 'tms1'])
            onesb = SB(es, 'onesb', [128, 784], BF16)
            vcs = SB(es, 'vcs', [128, NT], F32)
            P.op('pool', lambda e: e.memset(onesb[:], 1.0), writes=['onesb'])
            P.dma('sp', lambda e: e.dma_start(out=vcs[:], in_=validc[:, :]), writes=['vcs'])
            for tb in range(NQB):
                xT, xTn = xTr.next()
                for i in range(4):
                    tok0 = tb * 512 + i * 128
                    xi, xin_n = xin.next()
                    xb, xbn = xbf.next()
                    pt, ptn = pT.next()
                    P.dma('sp', lambda e, xi=xi, tok0=tok0: e.dma_start(out=xi[:], in_=x[tok0:tok0 + 128, :]), writes=[xin_n])
                    P.op('pool', lambda e, xi=xi, xb=xb: e.tensor_copy(xb[:], xi[:]), reads=[xin_n], writes=[xbn])
                    for kc in range(8):
                        tr(pt[:, kc * 128:(kc + 1) * 128], xb[:, kc * 128:(kc + 1) * 128], [xbn], [ptn])
                    P.op('act', lambda e, pt=pt, xT=xT, i=i: e.copy(
                        xT[:, :, i * 128:(i + 1) * 128], pt[:].rearrange("p (a b) -> p a b", b=128)), reads=[ptn], writes=[xTn])
                for oc in range(32 if tb >= QB0 else 8):
                    pa, pan = pacc.next()
                    for kc in range(8):
                        mm(pa[:, :], wf[:, kc, oc * 128:(oc + 1) * 128], xT[:, kc, :], kc == 0, kc == 7, ['wf', xTn], [pan])
                    st, stn = stg.next()
                    if oc < 8:
                        P.op('dve', lambda e, st=st, pa=pa: e.tensor_copy(st[:], pa[:]), reads=[pan], writes=[stn])
                    elif oc < 16:
                        P.op('dve', lambda e, st=st, pa=pa: e.tensor_scalar(st[:], pa[:], 0.125, None, ALU.mult),
                             reads=[pan], writes=[stn])
                    else:
                        P.op('act', lambda e, st=st, pa=pa: e.activation(st[:], pa[:], AF.Sigmoid), reads=[pan], writes=[stn])
                    P.dma('sp', lambda e, st=st, oc=oc, tb=tb: e.dma_start(out=FM[oc, :, tb * 512:(tb + 1) * 512], in_=st[:]),
                          reads=[stn], writes=['FM%d' % oc])
                for i in range(4):
                    tok0 = tb * 512 + i * 128
                    pa, pan = pacc.next()
                    pb_, pbn = pacc.next()
                    for kc in range(8):
                        mm(pa[:, :], xT[:, kc, i * 128:(i + 1) * 128], wt[:, kc, 0:512], kc == 0, kc == 7, ['wt', xTn], [pan])
                    for kc in range(8):
                        mm(pb_[:, 0:280], xT[:, kc, i * 128:(i + 1) * 128], wt[:, kc, 512:792], kc == 0, kc == 7, ['wt', xTn], [pbn])
                    tm, tmn = tms.next()
                    ng, ngn = ngs.next()
                    P.op('dve', lambda e, tm=tm, pa=pa: e.tensor_copy(
                        tm[:, 0:264].rearrange("p (a b) -> p a b", b=66)[:, :, 0:64],
                        pa[:, 0:256].rearrange("p (a b) -> p a b", b=64)), reads=[pan], writes=[tmn])
                    P.op('dve', lambda e, tm=tm, pa=pa: e.tensor_copy(
                        tm[:, 264:524].rearrange("p (a b) -> p a b", b=130)[:, :, 0:128],
                        pa[:, 256:512].rearrange("p (a b) -> p a b", b=128)), reads=[pan, tmn], writes=[tmn])
                    P.op('act', lambda e, tm=tm, pb_=pb_: e.copy(
                        tm[:, 524:784].rearrange("p (a b) -> p a b", b=130)[:, :, 0:128],
                        pb_[:, 0:256].rearrange("p (a b) -> p a b", b=128)), reads=[pbn, tmn], writes=[tmn])
                    P.op('act', lambda e, ng=ng, pb_=pb_: e.copy(ng[:], pb_[:, 256:280]), reads=[pbn], writes=[ngn])
                    tix = tb * 4 + i
                    P.op('dve', lambda e, tm=tm, tix=tix: e.tensor_scalar(
                        tm[:, 0:264].rearrange("p (a b) -> p a b", b=66)[:, :, 64:65],
                        onesb[:, 0:264].rearrange("p (a b) -> p a b", b=66)[:, :, 64:65], vcs[:, tix:tix + 1], None, ALU.mult),
                        reads=['onesb', 'vcs', tmn], writes=[tmn])
                    P.op('dve', lambda e, tm=tm, tix=tix: e.tensor_scalar(
                        tm[:, 264:784].rearrange("p (a b) -> p a b", b=130)[:, :, 128:129],
                        onesb[:, 264:784].rearrange("p (a b) -> p a b", b=130)[:, :, 128:129], vcs[:, tix:tix + 1], None, ALU.mult),
                        reads=['onesb', 'vcs', tmn], writes=[tmn])
                    P.dma('sp', lambda e, tm=tm, tok0=tok0: e.dma_start(out=VTM[tok0:tok0 + 128, :], in_=tm[:]),
                          reads=[tmn], writes=['VTM'])
                    P.dma('sp', lambda e, ng=ng, tok0=tok0: e.dma_start(out=NGs[tok0:tok0 + 128, :], in_=ng[:]),
                          reads=[ngn], writes=['NG'])
            P.barrier()
            P.emit()
        es01.close()
        if DEBUG == 'p1':
            return nc, 1
        FMALL = ['FM%d' % i for i in range(32)]

        with contextlib.ExitStack() as es:
            kin = SB(es, 'kin', [128, S], BF16)
            vin = SB(es, 'vin', [128, S], BF16)
            w1s = [SB(es, 'w1s%d' % i, [128, 32, 256], BF16) for i in range(2)]
            w1st = Rot('w1st', [SB(es, 'w1st%d' % i, [128, 8, 256], F32) for i in range(2)])
            w2f = SB(es, 'w2f', [128, 2, 2, 64], F32)
            w2b = SB(es, 'w2b', [128, 2, 2, 64], BF16)
            pef = SB(es, 'pef', [64, 2, 32], F32)
            peb16 = SB(es, 'peb16', [64, 2, 32], BF16)
            pebias = SB(es, 'pebias', [128, 4], F32)
            hT = [SB(es, 'hT%d' % i, [128, 256], BF16) for i in range(8)]
            cvs = SB(es, 'cvs', [128, 2], F32)
            msf = SB(es, 'msf', [128, 2, 64], F32)
            pc = Rot('pc', [PSB(es, 'pc%d' % i) for i in range(4)])
            P.dma('sp', lambda e: e.dma_start(out=kin[:], in_=FM[2]), reads=FMALL, writes=['kin'])
            P.dma('sp', lambda e: e.dma_start(out=vin[:], in_=FM[3]), reads=FMALL, writes=['vin'])
            for kv, w1 in enumerate((w1k, w1v)):
                src = w1.rearrange("(l d) h -> d l h", d=64)
                for q4 in range(4):
                    ws_, wsn = w1st.next()
                    for half in range(2):
                        P.dma('sp', lambda e, ws_=ws_, half=half, q4=q4, src=src: e.dma_start(
                            out=ws_[half * 64:(half + 1) * 64, :, :], in_=src[:, q4 * 8:(q4 + 1) * 8, :]), writes=[wsn])
                    P.op('pool' if q4 % 2 else 'dve', lambda e, ws_=ws_, kv=kv, q4=q4: e.tensor_copy(
                        w1s[kv][:, q4 * 8:(q4 + 1) * 8, :], ws_[:]), reads=[wsn], writes=['w1s%d' % kv])
            for kv, w2 in enumerate((w2k, w2v)):
                P.dma('sp', lambda e, kv=kv, w2=w2: e.dma_start(out=w2f[:, kv, :, :], in_=w2.rearrange("(c p) d -> p c d", p=128)),
                      writes=['w2f'])
            P.op('dve', lambda e: e.tensor_copy(w2b[:], w2f[:]), reads=['w2f'], writes=['w2b'])
            P.dma('sp', lambda e: e.dma_start(out=pef[:, 0, :], in_=pekT[:, :]), writes=['pef'])
            P.dma('sp', lambda e: e.dma_start(out=pef[:, 1, :], in_=pevT[:, :]), writes=['pef'])
            P.op('dve', lambda e: e.tensor_copy(peb16[:], pef[:]), reads=['pef'], writes=['peb16'])
            P.dma('sp', lambda e: e.dma_start(out=cvs[:], in_=cvalid[:, :]), writes=['cvs'])
            P.dma('sp', lambda e: e.dma_start(out=msf[:], in_=mslc.rearrange("(c p) j -> p c j", p=128)), writes=['msf'])
            for t_ in hT:
                P.op('pool', lambda e, t_=t_: e.memset(t_[:], 0.0), writes=['hT'])
            P.op('pool', lambda e: e.memset(KCT[:], 0.0), writes=['KCT'])
            P.op('pool', lambda e: e.memset(VCA[:], 0.0), writes=['VCA'])
            for kv in range(2):
                for hc in range(2):
                    pp_, ppn = pc.next()
                    for l in range(32):
                        mm(pp_[:, 0:1], w1s[kv][0:64, l, hc * 128:(hc + 1) * 128], peb16[0:64, kv, l:l + 1], l == 0, l == 31,
                           ['w1s%d' % kv, 'peb16'], [ppn])
                    P.op('dve', lambda e, pp_=pp_, kv=kv, hc=hc: e.tensor_copy(pebias[:, kv * 2 + hc:kv * 2 + hc + 1], pp_[:, 0:1]),
                         reads=[ppn], writes=['pebias'])
            for kv, src_t, srcn in ((0, kin, 'kin'), (1, vin, 'vin')):
                for g in range(2):
                    for hc in range(2):
                        pp_, ppn = pc.next()
                        for l in range(32):
                            if l < 16:
                                rhs = src_t[64 * g:64 * g + 64, :].rearrange("p (n s) -> p n s", s=16)[:, 0:255, l]
                            else:
                                rhs = src_t[64 * g:64 * g + 64, 16:S].rearrange("p (n s) -> p n s", s=16)[:, 0:255, l - 16]
                            mm(pp_[:, 0:255], w1s[kv][64 * g:64 * g + 64, l, hc * 128:(hc + 1) * 128], rhs, l == 0, l == 31,
                               ['w1s%d' % kv, srcn], [ppn])
                        ht = hT[kv * 4 + g * 2 + hc]
                        P.op('act', lambda e, ht=ht, pp_=pp_, kv=kv, hc=hc: e.activation(
                            ht[:, 0:255], pp_[:, 0:255], AF.Gelu_apprx_tanh, bias=pebias[:, kv * 2 + hc:kv * 2 + hc + 1]),
                            reads=[ppn, 'pebias', 'hT'], writes=['hT'])
            for g in range(2):
                pp_, ppn = pc.next()
                for hc in range(2):
                    mm(pp_[0:64, 0:255], w2b[:, 0, hc, :], hT[g * 2 + hc][:, 0:255], hc == 0, hc == 1, ['w2b', 'hT'], [ppn])
                P.op('dve', lambda e, pp_=pp_, g=g: e.tensor_copy(KCT[0:64, g, 0:255], pp_[0:64, 0:255]),
                     reads=[ppn, 'KCT'], writes=['KCT'])
                for nt in range(2):
                    pp2, pp2n = pc.next()
                    for hc in range(2):
                        mm(pp2[:, 0:64], hT[4 + g * 2 + hc][:, nt * 128:(nt + 1) * 128], w2b[:, 1, hc, :], hc == 0, hc == 1,
                           ['w2b', 'hT'], [pp2n])
                    P.op('dve', lambda e, pp2=pp2, g=g, nt=nt: e.tensor_scalar(
                        VCA[:, nt, g, 0:64], pp2[:, 0:64], cvs[:, nt:nt + 1], None, ALU.mult),
                        reads=[pp2n, 'cvs', 'VCA'], writes=['VCA'])
                    P.op('dve', lambda e, g=g, nt=nt: e.tensor_copy(VCA[:, nt, g, 64:65], cvs[:, nt:nt + 1]),
                         reads=['cvs', 'VCA'], writes=['VCA'])
                    P.op('dve', lambda e, g=g, nt=nt: e.tensor_copy(VCA[:, nt, g, 65:129], msf[:, nt, :]),
                         reads=['msf', 'VCA'], writes=['VCA'])
            if DEBUG == 'p2c':
                dt_ = SB(es, 'dbgt', [128, 4096], F32)
                P.op('pool', lambda e: e.memset(dt_[:], 0.0), writes=['dbgt'])
                P.op('dve', lambda e: e.tensor_copy(dt_[0:64, 0:512], KCT[:].rearrange("p a b -> p (a b)")), reads=['KCT', 'dbgt'], writes=['dbgt'])
                P.op('dve', lambda e: e.tensor_copy(dt_[:, 512:1032], VCA[:].rearrange("p a b c -> p (a b c)")), reads=['VCA', 'dbgt'], writes=['dbgt'])
                P.dma('sp', lambda e: e.dma_start(out=DBG[:, :], in_=dt_[:]), reads=['dbgt'], writes=['DBG'])
            P.barrier()
            P.emit()
        if DEBUG == 'p2c':
            return nc, 1

        def attn_pass(Sb, ACCb, PTr, accr, units, nq, keys, w):
            nu = len(units)
            nqt = nq // 128
            per = 512 // w
            nslots = nu * nqt
            nbank = (nslots + per - 1) // per
            for b in range(nbank):
                mm(ACCb[b][:, :], zb[:, 0:128], zb[:, 0:512], True, True, ['zb'], ['ACC'])
            pend = None

            def do_pv(items, last):
                for (pt, ptn, key, u) in items:
                    ilo_, ihi_ = key.get('ir', (0, nqt))
                    for i in range(ilo_, ihi_):
                        s_ = i * nu + u
                        mm(ACCb[s_ // per][:, (s_ % per) * w:(s_ % per) * w + w], pt[:, i * 128:(i + 1) * 128], key['v'](u),
                           False, last, [ptn] + key['reads'], ['ACC'])
            cnt = 0
            for ki, key in enumerate(keys):
                items = []
                for u in range(nu):
                    sbk = Sb[cnt % 4]
                    sbn = 'S%d' % (cnt % 4)
                    cnt += 1
                    bias = key['bias'](u) if key['bias'] is not None else None
                    ilo_, ihi_ = key.get('ir', (0, nqt))
                    ca, cb2 = ilo_ * 128, ihi_ * 128
                    mm(sbk[:, ca:cb2], key['lhsT'](u), units[u]['rhs'][:, ca:cb2], True, bias is None,
                       [units[u]['res']] + key['reads'], [sbn])
                    if bias is not None:
                        mm(sbk[:, ca:cb2], identb[:], bias[0][:, ca:cb2], False, True, ['identb', bias[1]], [sbn])
                    pt, ptn = PTr.next()
                    P.op('act', lambda e, pt=pt, sbk=sbk, ca=ca, cb2=cb2: e.activation(pt[:, ca:cb2], sbk[:, ca:cb2], AF.Exp),
                         reads=[sbn], writes=[ptn])
                    items.append((pt, ptn, key, u))
                if pend is not None:
                    do_pv(pend, False)
                pend = items
            do_pv(pend, True)
            asb, asbn = accr.next()
            for b in range(nbank):
                used = min(per, nslots - b * per) * w
                P.op('dve', lambda e, b=b, used=used, asb=asb: e.tensor_copy(asb[:, b, 0:used], ACCb[b][:, 0:used]),
                     reads=['ACC'], writes=[asbn])
            return asb, asbn, per, nslots, nbank

        def recip_sums(asb, asbn, rs, per, nslots, nbank, w, sumcol):
            for b in range(nbank):
                nsb = min(per, nslots - b * per)
                P.op('dve', lambda e, b=b, nsb=nsb: e.tensor_scalar(
                    rs[:, b * per:b * per + nsb], asb[:, b, 0:nsb * w].rearrange("p (s c) -> p s c", c=w)[:, :, sumcol],
                    1e-20, None, ALU.max), reads=[asbn, 'rs'], writes=['rs'])
            P.op('dve', lambda e: e.reciprocal(rs[:, 0:nslots], rs[:, 0:nslots]), reads=['rs'], writes=['rs'])

        with contextlib.ExitStack() as es:
            Ub = SB(es, 'Ub', [128, 8, 6, 512], BF16)
            Utf = SB(es, 'Utf', [128, 3072], F32)
            Utb = SB(es, 'Utb', [128, 6, 512], BF16)
            Wc = SB(es, 'Wc', [128, 3, 512], BF16)
            KS = SB(es, 'KS', [128, 2, S], BF16)
            KW = SB(es, 'KW', [64, 2, S], BF16)
            VSW = SB(es, 'VSW', [128, NT, 264], BF16)
            amk = SB(es, 'amk', [128, NT * 64], F32)
            QSr = [Rot('QS%d' % u, [SB(es, 'QS%d_%d' % (u, i), [128, 512], BF16) for i in range(2)]) for u in range(4)]
            PTr = Rot('PT', [SB(es, 'PT%d' % i, [128, 512], BF16) for i in range(8)])
            cbr = Rot('cb', [SB(es, 'cb%d' % i, [128, 512], BF16) for i in range(4)])
            accr = Rot('asb', [SB(es, 'asb%d' % i, [128, 3, 512], F32) for i in range(2)])
            rs = SB(es, 'rs', [128, 16], F32)
            coef = SB(es, 'coef', [128, 16], F32)
            gsb = Rot('gsb', [SB(es, 'gsb%d' % i, [128, 4, 24], F32) for i in range(2)])
            ynsa = SB(es, 'ynsa', [128, 4, 512], F32)
            ynb = SB(es, 'ynb', [128, 4, 512], BF16)
            psl = SB(es, 'psl', [128, 4, 64], F32)
            sc = SB(es, 'sc', [128, 64], F32)
            sc2 = SB(es, 'sc2', [128, 64], F32)
            m8 = SB(es, 'm8', [128, 16], F32)
            nmb = SB(es, 'nmb', [128, 64], BF16)
            nmT = Rot('nmT', [SB(es, 'nmT%d' % i, [64, 512], BF16) for i in range(2)])
            ystg = Rot('ystg', [SB(es, 'ystg%d' % i, [128, 512], BF16) for i in range(2)])
            Sb = [PSB(es, 'Sb%d' % i) for i in range(4)]
            ACCb = [PSB(es, 'ACCb%d' % i) for i in range(3)]
            pm = PSB(es, 'pm', BF16)
            for cch in range(3):
                P.dma('sp', lambda e, cch=cch: e.dma_start(out=Utf[:, cch * 1024:(cch + 1) * 1024],
                                                           in_=utmpl[:, cch * 1024:(cch + 1) * 1024]), writes=['Utf'])
            P.op('dve', lambda e: e.tensor_copy(Utb[:].rearrange("p a b -> p (a b)"), Utf[:]), reads=['Utf'], writes=['Utb'])
            for h in range(8):
                P.op('pool', lambda e, h=h: e.tensor_copy(Ub[:, h, :, :], Utb[:]), reads=['Utb'], writes=['Ub%d' % h])
                for (m, i, base) in [(1, 0, 0), (2, 1, 0), (3, 2, 0), (4, 3, 0), (0, 0, 128), (1, 1, 128), (2, 2, 128), (3, 3, 128),
                                     (5, 0, 128)]:
                    P.dma('sp', lambda e, h=h, m=m, i=i, base=base: e.dma_start(
                        out=Ub[:, h, m, i * 128:(i + 1) * 128], in_=toep(h, base, 1, 128)), reads=['FVR'], writes=['Ub%d' % h])
            P.dma('sp', lambda e: e.dma_start(out=Utf[:, 0:1536], in_=wctmpl[:, :]), reads=['Utb'], writes=['Utf'])
            P.op('dve', lambda e: e.tensor_copy(Wc[:].rearrange("p a b -> p (a b)"), Utf[:, 0:1536]), reads=['Utf'], writes=['Wc'])
            for g in range(2):
                P.dma('sp', lambda e, g=g: e.dma_start(out=KS[0:64, g, :], in_=FM[0, 64 * g:64 * g + 64, :]), reads=FMALL, writes=['KS'])
                P.dma('sp', lambda e, g=g: e.dma_start(out=KW[0:64, g, :], in_=FM[1, 64 * g:64 * g + 64, :]), reads=FMALL, writes=['KW'])
            for hf in range(2):
                P.dma('sp', lambda e, hf=hf: e.dma_start(out=Utf[64:128, 0:2048], in_=eall[:, hf * 2048:(hf + 1) * 2048]),
                      reads=['Wc'], writes=['Utf'])
                for g in range(2):
                    P.op('dve' if g else 'pool', lambda e, g=g, hf=hf: e.tensor_copy(
                        KS[64:128, g, hf * 2048:(hf + 1) * 2048], Utf[64:128, 0:2048]), reads=['Utf', 'KS'], writes=['KS'])
            for q4 in range(4):
                P.dma('sp', lambda e, q4=q4: e.dma_start(
                    out=VSW[:, q4 * 8:(q4 + 1) * 8, :],
                    in_=VTM[q4 * 1024:(q4 + 1) * 1024, 0:264].rearrange("(t p) c -> p t c", p=128)), reads=['VTM'], writes=['VSW'])
            P.dma('sp', lambda e: e.dma_start(out=amk[:], in_=addmask[:, :]), writes=['amk'])

            for qb in range(QB0, NQB):
                q0 = qb * 512
                nq = 512
                G0 = qb * 4
                gs_, gsn = gsb.next()
                P.dma('sp', lambda e, gs_=gs_, q0=q0: e.dma_start(
                    out=gs_[:], in_=NGs[q0:q0 + 512, :].rearrange("(i p) c -> p i c", p=128)), reads=['NG'], writes=[gsn])
                P.op('act', lambda e, gs_=gs_: e.activation(gs_[:], gs_[:], AF.Sigmoid), reads=[gsn], writes=[gsn])
                for g in range(2):
                    units = []
                    for u in range(4):
                        qs, qsn = QSr[u].next()
                        P.dma('sp', lambda e, qs=qs, u=u, g=g, q0=q0: e.dma_start(
                            out=qs[0:64, :], in_=FM[8 + u, 64 * g:64 * g + 64, q0:q0 + 512]), reads=FMALL, writes=[qsn])
                        units.append(dict(t=qs, res=qsn))
                    for p2 in range(2):
                        us = [dict(rhs=units[2 * p2 + ul]['t'][0:64, :], res=units[2 * p2 + ul]['res']) for ul in range(2)]
                        keys = []
                        for nt in range(2):
                            typ = cmp_tile_type(q0, nq, nt)
                            if typ == 'skip':
                                continue
                            bias_tiles = {}
                            if typ == 'bias':
                                for ul in range(2):
                                    h = 4 * g + 2 * p2 + ul
                                    cb, cbn = cbr.next()
                                    P.dma('sp', lambda e, cb=cb, h=h, q0=q0, nt=nt: e.dma_start(
                                        out=cb[:, :], in_=toep(h, q0 - 16 * 128 * nt - 31, 16, 512)), reads=['FVR'], writes=[cbn])
                                    bias_tiles[ul] = (cb[:, :], cbn)
                            keys.append(dict(
                                lhsT=lambda u, nt=nt, g=g: KCT[0:64, g, nt * 128:(nt + 1) * 128],
                                bias=(lambda u, bt=bias_tiles: bt[u]) if typ == 'bias' else None,
                                v=lambda u, nt=nt, g=g: VCA[:, nt, g, 0:129], reads=['KCT', 'VCA']))
                        asb, asbn, per, nslots, nbank = attn_pass(Sb, ACCb, PTr, accr, us, nq, keys, 129)
                        recip_sums(asb, asbn, rs, per, nslots, nbank, 129, 64)
                        gv = gs_[:, :, 12 * g + 6 * p2:12 * g + 6 * p2 + 6].rearrange("p i (u c) -> p i u c", c=3)[:, :, :, 0]
                        P.op('dve', lambda e, gv=gv: e.tensor_tensor(coef[:, 0:8].rearrange("p (i u) -> p i u", u=2),
                                                                     rs[:, 0:8].rearrange("p (i u) -> p i u", u=2), gv, ALU.mult),
                             reads=['rs', gsn, 'coef'], writes=['coef'])
                        for i in range(4):
                            for ul in range(2):
                                s_ = i * 2 + ul
                                h = 4 * g + 2 * p2 + ul
                                b_, c_ = s_ // per, (s_ % per) * 129
                                P.op('dve', lambda e, asb=asb, b_=b_, c_=c_, i=i, h=h, s_=s_: e.tensor_scalar(
                                    ynsa[:, i, h * 64:(h + 1) * 64], asb[:, b_, c_:c_ + 64], coef[:, s_:s_ + 1], None, ALU.mult),
                                    reads=[asbn, 'coef', 'ynsa'], writes=['ynsa'])
                                if p2 == 0 and ul == 0:
                                    P.op('dve', lambda e, asb=asb, b_=b_, c_=c_, i=i, s_=s_: e.tensor_scalar(
                                        psl[:, i, :], asb[:, b_, c_ + 65:c_ + 129], rs[:, s_:s_ + 1], None, ALU.mult),
                                        reads=[asbn, 'rs', 'psl'], writes=['psl'])
                                else:
                                    P.op('dve', lambda e, asb=asb, b_=b_, c_=c_, i=i, s_=s_: e.scalar_tensor_tensor(
                                        out=psl[:, i, :], in0=asb[:, b_, c_ + 65:c_ + 129], scalar=rs[:, s_:s_ + 1], in1=psl[:, i, :],
                                        op0=ALU.mult, op1=ALU.add), reads=[asbn, 'rs', 'psl'], writes=['psl'])
                    nm, nmn = nmT.next()
                    for i in range(4):
                        gt = G0 + i
                        P.op('dve', lambda e, i=i, gt=gt: e.tensor_tensor(sc[:], psl[:, i, :], amk[:, gt * 64:(gt + 1) * 64], ALU.add),
                             reads=['psl', 'amk', 'sc'], writes=['sc'])
                        P.op('dve', lambda e: e.max(m8[:, 0:8], sc[:]), reads=['sc', 'm8'], writes=['m8'])
                        P.op('dve', lambda e: e.match_replace(sc2[:], m8[:, 0:8], sc[:], -1e30), reads=['sc', 'm8', 'sc2'], writes=['sc2'])
                        P.op('dve', lambda e: e.max(m8[:, 8:16], sc2[:]), reads=['sc2', 'm8'], writes=['m8'])
                        P.op('dve', lambda e: e.tensor_scalar(nmb[:], sc[:], m8[:, 15:16], NEG, ALU.is_lt, ALU.mult),
                             reads=['sc', 'm8', 'nmb'], writes=['nmb'])
                        tr(pm[0:64, i * 128:(i + 1) * 128], nmb[:, :], ['nmb'], ['pm'])
                    P.op('act', lambda e, nm=nm: e.copy(nm[:, :], pm[0:64, 0:512]), reads=['pm'], writes=[nmn])
                    for u in range(4):
                        P.dma('sp', lambda e, u=u, nm=nm, units=units: e.dma_start(out=units[u]['t'][64:128, :], in_=nm[:, :]),
                              reads=[nmn, units[u]['res']], writes=[units[u]['res']])
                    for br in (1, 2):
                        keys = []
                        if br == 1:
                            us = [dict(rhs=units[u]['t'][:, :], res=units[u]['res']) for u in range(4)]
                            for kt in range(G0 + 4):
                                m = kt - (G0 - 1)
                                keys.append(dict(
                                    ir=(max(0, m - 1), 4),
                                    lhsT=lambda u, kt=kt, g=g: KS[:, g, kt * 128:(kt + 1) * 128],
                                    bias=(lambda u, m=m, g=g: (Ub[:, 4 * g + u, m, :], 'Ub%d' % (4 * g + u))) if m >= 0 else None,
                                    v=lambda u, kt=kt, g=g: VSW[:, kt, g * 66:g * 66 + 65], reads=['KS', 'VSW']))
                        else:
                            us = [dict(rhs=units[u]['t'][0:64, :], res=units[u]['res']) for u in range(4)]
                            for jj in range(8):
                                kt = G0 - 4 + jj
                                if kt < 0:
                                    continue
                                if jj < 3:
                                    bf_ = lambda u, jj=jj: (Wc[:, jj, :], 'Wc')
                                elif jj == 3:
                                    bf_ = lambda u, g=g: (Ub[:, 4 * g + u, 5, :], 'Ub%d' % (4 * g + u))
                                else:
                                    bf_ = lambda u, g=g, jj=jj: (Ub[:, 4 * g + u, jj - 3, :], 'Ub%d' % (4 * g + u))
                                keys.append(dict(
                                    ir=(max(0, jj - 4), min(3, jj) + 1),
                                    lhsT=lambda u, kt=kt, g=g: KW[0:64, g, kt * 128:(kt + 1) * 128], bias=bf_,
                                    v=lambda u, kt=kt, g=g: VSW[:, kt, 132 + g * 66:132 + g * 66 + 65], reads=['KW', 'VSW']))
                        asb, asbn, per, nslots, nbank = attn_pass(Sb, ACCb, PTr, accr, us, nq, keys, 65)
                        recip_sums(asb, asbn, rs, per, nslots, nbank, 65, 64)
                        gv = gs_[:, :, 12 * g:12 * g + 12].rearrange("p i (u c) -> p i u c", c=3)[:, :, :, br]
                        P.op('dve', lambda e, gv=gv: e.tensor_tensor(coef[:, 0:16].rearrange("p (i u) -> p i u", u=4),
                                                                     rs[:, 0:16].rearrange("p (i u) -> p i u", u=4), gv, ALU.mult),
                             reads=['rs', gsn, 'coef'], writes=['coef'])
                        for i in range(4):
                            for u in range(4):
                                s_ = i * 4 + u
                                h = 4 * g + u
                                b_, c_ = s_ // per, (s_ % per) * 65
                                P.op('dve', lambda e, asb=asb, b_=b_, c_=c_, i=i, h=h, s_=s_: e.scalar_tensor_tensor(
                                    out=ynsa[:, i, h * 64:(h + 1) * 64], in0=asb[:, b_, c_:c_ + 64], scalar=coef[:, s_:s_ + 1],
                                    in1=ynsa[:, i, h * 64:(h + 1) * 64], op0=ALU.mult, op1=ALU.add),
                                    reads=[asbn, 'coef', 'ynsa'], writes=['ynsa'])
                P.op('pool', lambda e: e.tensor_copy(ynb[:], ynsa[:]), reads=['ynsa', 'ynb'], writes=['ynb'])
                for c in range(4):
                    for i in range(4):
                        tr(pm[:, i * 128:(i + 1) * 128], ynb[:, i, c * 128:(c + 1) * 128], ['ynb'], ['pm'])
                    ys, ysn = ystg.next()
                    P.op('act', lambda e, ys=ys: e.copy(ys[:, :], pm[:, 0:512]), reads=['pm'], writes=[ysn])
                    P.dma('sp', lambda e, ys=ys, c=c, q0=q0: e.dma_start(out=YT[c, :, q0:q0 + 512], in_=ys[:, :]),
                          reads=[ysn], writes=['YT'])
            P.barrier()
            P.emit()
        if DEBUG == 'p2a':
            return nc, 1

        with contextlib.ExitStack() as es:
            Ud = SB(es, 'Ud', [128, 4, 6, 512], BF16)
            Utf = SB(es, 'Utf2', [128, 3072], F32)
            Utb = SB(es, 'Utb2', [128, 6, 512], BF16)
            KD = SB(es, 'KD', [128, 4, S], BF16)
            VD = SB(es, 'VD', [128, NT, 520], BF16)
            QAr = Rot('QA', [SB(es, 'QA%d' % i, [128, 512], BF16) for i in range(2)])
            QBr = Rot('QB', [SB(es, 'QB%d' % i, [128, 512], BF16) for i in range(2)])
            PTr = Rot('PT', [SB(es, 'PTd%d' % i, [128, 512], BF16) for i in range(8)])
            accr = Rot('asb', [SB(es, 'asbd%d' % i, [128, 3, 512], F32) for i in range(2)])
            rs = SB(es, 'rsd', [128, 16], F32)
            cf = SB(es, 'cfd', [128, 16], F32)
            lamt = SB(es, 'lamt', [128, 256], F32)
            lpr = SB(es, 'lpr', [128, 128], F32)
            lsm = SB(es, 'lsm', [128, 4], F32)
            neglam = SB(es, 'neglam', [128, 1], F32)
            subg = SB(es, 'subg', [128, 128], F32)
            tmp1 = SB(es, 'tmp1', [128, 4, 128], F32)
            od = SB(es, 'od', [128, 4, 128], F32)
            junk = SB(es, 'junk', [128, 128], F32)
            ss = SB(es, 'ssd', [128, 16], F32)
            yd = SB(es, 'yd', [128, 4, 512], F32)
            ydb = SB(es, 'ydb', [128, 4, 512], BF16)
            ystg = Rot('ystg', [SB(es, 'ystgd%d' % i, [128, 512], BF16) for i in range(2)])
            Sb = [PSB(es, 'Sbd%d' % i) for i in range(4)]
            ACCb = [PSB(es, 'ACCd%d' % i) for i in range(3)]
            pm = PSB(es, 'pmd', BF16)
            for cch in range(3):
                P.dma('sp', lambda e, cch=cch: e.dma_start(out=Utf[:, cch * 1024:(cch + 1) * 1024],
                                                           in_=utmpl[:, cch * 1024:(cch + 1) * 1024]), writes=['Utf'])
            P.op('dve', lambda e: e.tensor_copy(Utb[:].rearrange("p a b -> p (a b)"), Utf[:]), reads=['Utf'], writes=['Utb'])
            for hd in range(4):
                P.op('pool', lambda e, hd=hd: e.tensor_copy(Ud[:, hd, :, :], Utb[:]), reads=['Utb'], writes=['Ud%d' % hd])
                for (m, i, base) in [(1, 0, 0), (2, 1, 0), (3, 2, 0), (4, 3, 0), (0, 0, 128), (1, 1, 128), (2, 2, 128), (3, 3, 128)]:
                    P.dma('sp', lambda e, hd=hd, m=m, i=i, base=base: e.dma_start(
                        out=Ud[:, hd, m, i * 128:(i + 1) * 128], in_=toep(8 + hd, base, 1, 128)), reads=['FVR'], writes=['Ud%d' % hd])
                P.dma('sp', lambda e, hd=hd: e.dma_start(out=KD[:, hd, :], in_=FM[4 + hd]), reads=FMALL, writes=['KD'])
            for q4 in range(4):
                P.dma('sp', lambda e, q4=q4: e.dma_start(
                    out=VD[:, q4 * 8:(q4 + 1) * 8, :],
                    in_=VTM[q4 * 1024:(q4 + 1) * 1024, 264:784].rearrange("(t p) c -> p t c", p=128)), reads=['VTM'], writes=['VD'])
            for i_ in range(2):
                P.op('pool', lambda e, t_=QAr.tiles[i_]: e.memset(t_[64:128, :], 0.0), writes=['QA%d' % i_])
                P.op('pool', lambda e, t_=QBr.tiles[i_]: e.memset(t_[0:64, :], 0.0), writes=['QB%d' % i_])
            P.dma('sp', lambda e: e.dma_start(out=lamt[:], in_=lam_in.partition_broadcast(128)), writes=['lamt'])
            P.dma('sp', lambda e: e.dma_start(out=subg[:], in_=subln.partition_broadcast(128)), writes=['subg'])
            P.op('dve', lambda e: e.tensor_tensor(lpr[:, 0:64], lamt[:, 0:64], lamt[:, 64:128], ALU.mult), reads=['lamt'], writes=['lpr'])
            P.op('dve', lambda e: e.tensor_tensor(lpr[:, 64:128], lamt[:, 128:192], lamt[:, 192:256], ALU.mult),
                 reads=['lamt', 'lpr'], writes=['lpr'])
            P.op('dve', lambda e: e.reduce_sum(lsm[:, 0:1], lpr[:, 0:64], AX.X), reads=['lpr'], writes=['lsm'])
            P.op('dve', lambda e: e.reduce_sum(lsm[:, 1:2], lpr[:, 64:128], AX.X), reads=['lpr', 'lsm'], writes=['lsm'])
            P.op('act', lambda e: e.activation(lsm[:, 2:4], lsm[:, 0:2], AF.Exp), reads=['lsm'], writes=['lsm'])
            P.op('dve', lambda e: e.tensor_tensor(neglam[:], lsm[:, 3:4], lsm[:, 2:3], ALU.subtract), reads=['lsm'], writes=['neglam'])
            P.op('dve', lambda e: e.tensor_scalar(neglam[:], neglam[:], -0.2, None, ALU.add), reads=['neglam'], writes=['neglam'])

            for qb in range(QB0, NQB):
                q0 = qb * 512
                nq = 512
                G0 = qb * 4
                for hd in range(4):
                    qa, qan = QAr.next()
                    qb_, qbn = QBr.next()
                    P.dma('sp', lambda e, qa=qa, hd=hd, q0=q0: e.dma_start(out=qa[0:64, :], in_=FM[12 + hd, 0:64, q0:q0 + 512]),
                          reads=FMALL + [qan], writes=[qan])
                    P.dma('sp', lambda e, qb_=qb_, hd=hd, q0=q0: e.dma_start(out=qb_[64:128, :], in_=FM[12 + hd, 64:128, q0:q0 + 512]),
                          reads=FMALL + [qbn], writes=[qbn])
                    us = [dict(rhs=qa[:, :], res=qan), dict(rhs=qb_[:, :], res=qbn)]
                    keys = []
                    for kt in range(G0 + 4):
                        m = kt - (G0 - 1)
                        keys.append(dict(
                            ir=(max(0, m - 1), 4),
                            lhsT=lambda u, kt=kt, hd=hd: KD[:, hd, kt * 128:(kt + 1) * 128],
                            bias=(lambda u, m=m, hd=hd: (Ud[:, hd, m, :], 'Ud%d' % hd)) if m >= 0 else None,
                            v=lambda u, kt=kt, hd=hd: VD[:, kt, hd * 130:hd * 130 + 129], reads=['KD', 'VD']))
                    asb, asbn, per, nslots, nbank = attn_pass(Sb, ACCb, PTr, accr, us, nq, keys, 129)
                    recip_sums(asb, asbn, rs, per, nslots, nbank, 129, 128)
                    P.op('dve', lambda e: e.tensor_scalar(cf[:, 0:8], rs[:, 0:8], neglam[:, 0:1], None, ALU.mult),
                         reads=['rs', 'neglam', 'cf'], writes=['cf'])
                    for i in range(4):
                        s1, s2 = 2 * i, 2 * i + 1
                        b1, c1 = s1 // per, (s1 % per) * 129
                        b2, c2 = s2 // per, (s2 % per) * 129
                        P.op('dve', lambda e, asb=asb, b1=b1, c1=c1, i=i, s1=s1: e.tensor_scalar(
                            tmp1[:, i, :], asb[:, b1, c1:c1 + 128], rs[:, s1:s1 + 1], None, ALU.mult),
                            reads=[asbn, 'rs', 'tmp1'], writes=['tmp1'])
                        P.op('dve', lambda e, asb=asb, b2=b2, c2=c2, i=i, s2=s2: e.scalar_tensor_tensor(
                            out=od[:, i, :], in0=asb[:, b2, c2:c2 + 128], scalar=cf[:, s2:s2 + 1], in1=tmp1[:, i, :],
                            op0=ALU.mult, op1=ALU.add), reads=[asbn, 'cf', 'tmp1', 'od'], writes=['od'])
                        P.op('act', lambda e, i=i: e.activation(junk[:, :], od[:, i, :], AF.Square, accum_out=ss[:, i:i + 1]),
                             reads=['od', 'junk', 'ss'], writes=['junk', 'ss'])
                    P.op('dve', lambda e: e.tensor_scalar(ss[:, 4:8], ss[:, 0:4], 1.0 / 128, EPS, ALU.mult, ALU.add),
                         reads=['ss'], writes=['ss'])
                    P.op('act', lambda e: e.activation(ss[:, 8:12], ss[:, 4:8], AF.Sqrt), reads=['ss'], writes=['ss'])
                    P.op('dve', lambda e: e.reciprocal(ss[:, 12:16], ss[:, 8:12]), reads=['ss'], writes=['ss'])
                    P.op('dve', lambda e: e.tensor_scalar(ss[:, 12:16], ss[:, 12:16], 0.8, None, ALU.mult), reads=['ss'], writes=['ss'])
                    for i in range(4):
                        P.op('dve', lambda e, i=i, hd=hd: e.scalar_tensor_tensor(
                            out=yd[:, i, hd * 128:(hd + 1) * 128], in0=od[:, i, :], scalar=ss[:, 12 + i:13 + i], in1=subg[:, :],
                            op0=ALU.mult, op1=ALU.mult), reads=['od', 'ss', 'subg', 'yd'], writes=['yd'])
                P.op('pool', lambda e: e.tensor_copy(ydb[:], yd[:]), reads=['yd', 'ydb'], writes=['ydb'])
                for c in range(4):
                    for i in range(4):
                        tr(pm[:, i * 128:(i + 1) * 128], ydb[:, i, c * 128:(c + 1) * 128], ['ydb'], ['pm'])
                    ys, ysn = ystg.next()
                    P.op('act', lambda e, ys=ys: e.copy(ys[:, :], pm[:, 0:512]), reads=['pm'], writes=[ysn])
                    P.dma('sp', lambda e, ys=ys, c=c, q0=q0: e.dma_start(out=YT[4 + c, :, q0:q0 + 512], in_=ys[:, :]),
                          reads=[ysn], writes=['YT'])
            P.barrier()
            P.emit()
        if DEBUG == 'p2b':
            return nc, 1

        def layernorm(es_, pfx, tiles, r, rn, gbc, bbc, o, on):
            st, mv, t1 = tiles
            P.op('dve', lambda e: e.bn_stats(st[:, 0:6], r[:, 0:512]), reads=[rn, pfx + 'st'], writes=[pfx + 'st'])
            P.op('dve', lambda e: e.bn_stats(st[:, 6:12], r[:, 512:1024]), reads=[rn, pfx + 'st'], writes=[pfx + 'st'])
            P.op('dve', lambda e: e.bn_aggr(mv[:, 0:2], st[:, 0:12]), reads=[pfx + 'st', pfx + 'mv'], writes=[pfx + 'mv'])
            P.op('dve', lambda e: e.tensor_scalar(mv[:, 2:3], mv[:, 1:2], 1.0, EPS, ALU.mult, ALU.add), reads=[pfx + 'mv'], writes=[pfx + 'mv'])
            P.op('act', lambda e: e.activation(mv[:, 3:4], mv[:, 2:3], AF.Sqrt), reads=[pfx + 'mv'], writes=[pfx + 'mv'])
            P.op('dve', lambda e: e.reciprocal(mv[:, 4:5], mv[:, 3:4]), reads=[pfx + 'mv'], writes=[pfx + 'mv'])
            P.op('dve', lambda e: e.tensor_scalar(t1[:, :], r[:, :], mv[:, 0:1], mv[:, 4:5], ALU.subtract, ALU.mult),
                 reads=[rn, pfx + 'mv', pfx + 't1'], writes=[pfx + 't1'])
            P.op('pool', lambda e: e.tensor_tensor(t1[:, :], t1[:, :], gbc[:, :], ALU.mult), reads=[pfx + 't1', pfx + 'g'], writes=[pfx + 't1'])
            P.op('pool', lambda e: e.tensor_tensor(o[:, :], t1[:, :], bbc[:, :], ALU.add), reads=[pfx + 't1', pfx + 'b', on], writes=[on])

        with contextlib.ExitStack() as es:
            wbn = SB(es, 'wbn', [128, 4, D], BF16)
            wbd = SB(es, 'wbd', [128, 4, D], BF16)
            wo = SB(es, 'wo', [128, 8, D], BF16)
            wst = Rot('wst3', [SB(es, 'wst3_%d' % i, [128, 1024], F32) for i in range(3)])
            g1 = SB(es, 'g1', [128, D], F32); b1 = SB(es, 'b1', [128, D], F32)
            ynT = Rot('ynT', [SB(es, 'ynT%d' % i, [128, 4, 512], BF16) for i in range(2)])
            ydT = Rot('ydT', [SB(es, 'ydT%d' % i, [128, 4, 512], BF16) for i in range(2)])
            gnr = Rot('gn', [SB(es, 'gn%d' % i, [128, 8, 512], BF16) for i in range(2)])
            gdr = Rot('gd', [SB(es, 'gd%d' % i, [128, 8, 512], BF16) for i in range(2)])
            mrg = Rot('mrg', [SB(es, 'mrg%d' % i, [128, 8, 512], BF16) for i in range(2)])
            ta = Rot('ta', [SB(es, 'ta%d' % i, [128, 512], F32) for i in range(2)])
            tb_ = Rot('tb', [SB(es, 'tb%d' % i, [128, 512], F32) for i in range(2)])
            xt = Rot('xt3', [SB(es, 'xt3_%d' % i, [128, D], F32) for i in range(2)])
            x1t = Rot('x1t', [SB(es, 'x1t%d' % i, [128, D], F32) for i in range(2)])
            x1b = Rot('x1b', [SB(es, 'x1b%d' % i, [128, D], BF16) for i in range(2)])
            x1s = Rot('x1s', [SB(es, 'x1s%d' % i, [128, 8, 128], BF16) for i in range(2)])
            lnt = (SB(es, 'l1st', [128, 12], F32), SB(es, 'l1mv', [128, 8], F32), SB(es, 'l1t1', [128, D], F32))
            pa3 = Rot('pa3', [PSB(es, 'pa3_%d' % i) for i in range(2)])
            pb3 = Rot('pb3', [PSB(es, 'pb3_%d' % i) for i in range(2)])
            pz3 = Rot('pz3', [PSB(es, 'pz3_%d' % i) for i in range(2)])
            pt3 = PSB(es, 'pt3', BF16)
            for kc in range(4):
                load_cast(es, lambda c0, c1, kc=kc: wbn[:, kc, c0:c1], lambda c0, c1, kc=kc: w_bn[kc * 128:(kc + 1) * 128, c0:c1],
                          128, D, [(0, 1024)], 'wbn', wst)
                load_cast(es, lambda c0, c1, kc=kc: wbd[:, kc, c0:c1], lambda c0, c1, kc=kc: w_bd[kc * 128:(kc + 1) * 128, c0:c1],
                          128, D, [(0, 1024)], 'wbd', wst)
            for kc in range(8):
                load_cast(es, lambda c0, c1, kc=kc: wo[:, kc, c0:c1], lambda c0, c1, kc=kc: w_out[kc * 128:(kc + 1) * 128, c0:c1],
                          128, D, [(0, 1024)], 'wo', wst)
            P.dma('sp', lambda e: e.dma_start(out=g1[:], in_=ln1g.partition_broadcast(128)), writes=['l1g'])
            P.dma('sp', lambda e: e.dma_start(out=b1[:], in_=ln1b.partition_broadcast(128)), writes=['l1b'])
            for tb in range(QB0, NQB):
                t0 = tb * 512
                yn, ynn = ynT.next(); ydt, ydn = ydT.next(); gn, gnn = gnr.next(); gd, gdn = gdr.next()
                P.dma('sp', lambda e, yn=yn, t0=t0: e.dma_start(out=yn[:], in_=YT[0:4, :, t0:t0 + 512].rearrange("c p t -> p c t")),
                      reads=['YT'], writes=[ynn])
                P.dma('sp', lambda e, ydt=ydt, t0=t0: e.dma_start(out=ydt[:], in_=YT[4:8, :, t0:t0 + 512].rearrange("c p t -> p c t")),
                      reads=['YT'], writes=[ydn])
                P.dma('sp', lambda e, gn=gn, t0=t0: e.dma_start(out=gn[:], in_=FM[16:24, :, t0:t0 + 512].rearrange("c p t -> p c t")),
                      reads=FMALL, writes=[gnn])
                P.dma('sp', lambda e, gd=gd, t0=t0: e.dma_start(out=gd[:], in_=FM[24:32, :, t0:t0 + 512].rearrange("c p t -> p c t")),
                      reads=FMALL, writes=[gdn])
                mg, mgn = mrg.next()
                for fc in range(8):
                    pa, pan = pa3.next(); pb_, pbn = pb3.next()
                    for kc in range(4):
                        mm(pa[:, :], wbn[:, kc, fc * 128:(fc + 1) * 128], yn[:, kc, :], kc == 0, kc == 3, ['wbn', ynn], [pan])
                    for kc in range(4):
                        mm(pb_[:, :], wbd[:, kc, fc * 128:(fc + 1) * 128], ydt[:, kc, :], kc == 0, kc == 3, ['wbd', ydn], [pbn])
                    t1, t1n = ta.next(); t2, t2n = tb_.next()
                    P.op('dve', lambda e, t1=t1, pa=pa, gn=gn, fc=fc: e.tensor_tensor(t1[:, :], pa[:, :], gn[:, fc, :], ALU.mult),
                         reads=[pan, gnn], writes=[t1n])
                    P.op('dve', lambda e, t2=t2, pb_=pb_, gd=gd, fc=fc: e.tensor_tensor(t2[:, :], pb_[:, :], gd[:, fc, :], ALU.mult),
                         reads=[pbn, gdn], writes=[t2n])
                    P.op('pool', lambda e, mg=mg, t1=t1, t2=t2, fc=fc: e.tensor_tensor(mg[:, fc, :], t1[:, :], t2[:, :], ALU.add),
                         reads=[t1n, t2n, mgn], writes=[mgn])
                for i in range(4):
                    tok0 = t0 + i * 128
                    xx, xxn = xt.next(); o1, o1n = x1t.next(); ob, obn = x1b.next(); xs_, xsn = x1s.next()
                    r_, rn = xx, xxn
                    P.dma('sp', lambda e, xx=xx, tok0=tok0: e.dma_start(out=xx[:], in_=x[tok0:tok0 + 128, :]), writes=[xxn])
                    for half in range(2):
                        pz, pzn = pz3.next()
                        for fc in range(8):
                            mm(pz[:, :], mg[:, fc, i * 128:(i + 1) * 128], wo[:, fc, half * 512:(half + 1) * 512], fc == 0, fc == 7,
                               [mgn, 'wo'], [pzn])
                        P.op('dve', lambda e, r_=r_, xx=xx, pz=pz, half=half: e.scalar_tensor_tensor(
                            out=r_[:, half * 512:(half + 1) * 512], in0=xx[:, half * 512:(half + 1) * 512], scalar=ALPHA, in1=pz[:, :],
                            op0=ALU.mult, op1=ALU.add), reads=[xxn, pzn], writes=[rn])
                    layernorm(es, 'l1', lnt, r_, rn, g1, b1, o1, o1n)
                    P.dma('sp', lambda e, o1=o1, tok0=tok0: e.dma_start(out=X1[tok0:tok0 + 128, :], in_=o1[:]), reads=[o1n], writes=['X1'])
                    P.op('act', lambda e, ob=ob, o1=o1: e.copy(ob[:], o1[:]), reads=[o1n, obn], writes=[obn])
                    for kc in range(8):
                        tr(pt3[:, kc * 128:(kc + 1) * 128], ob[:, kc * 128:(kc + 1) * 128], [obn], ['pt3'])
                    P.op('act', lambda e, xs_=xs_: e.copy(xs_[:], pt3[:].rearrange("p (a b) -> p a b", b=128)), reads=['pt3', xsn], writes=[xsn])
                    P.dma('sp', lambda e, xs_=xs_, tok0=tok0: e.dma_start(
                        out=X1T[:, :, tok0:tok0 + 128].rearrange("c p t -> p c t"), in_=xs_[:]), reads=[xsn], writes=['X1T'])
            P.barrier()
            P.emit()
        if DEBUG == 'p3':
            return nc, 1

        es4 = contextlib.ExitStack()
        wdn = SB(es4, 'wdn', [128, NFC, D], BF16)
        wpg = SB(es4, 'wpg', [128, 8, D], BF16)
        wpp = SB(es4, 'wpp', [128, 2, D], BF16)
        wst = Rot('wst4', [SB(es4, 'wst4_%d' % i, [128, 1024], F32) for i in range(3)])
        g2 = SB(es4, 'g2', [128, D], F32); b2 = SB(es4, 'b2', [128, D], F32)
        pre4 = []
        for j in range(NFC):
            load_cast(es4, lambda c0, c1, j=j: wdn[:, j, c0:c1], lambda c0, c1, j=j: w_down[j * 128:(j + 1) * 128, c0:c1],
                      128, D, [(0, 1024)], 'wdn', wst, defer=pre4)
        for kc in range(8):
            load_cast(es4, lambda c0, c1, kc=kc: wpg[:, kc, c0:c1], lambda c0, c1, kc=kc: w_pg[kc * 128:(kc + 1) * 128, c0:c1],
                      128, D, [(0, 1024)], 'wpg', wst, defer=pre4)
        for kc in range(2):
            load_cast(es4, lambda c0, c1, kc=kc: wpp[:, kc, c0:c1], lambda c0, c1, kc=kc: w_pp[kc * 128:(kc + 1) * 128, c0:c1],
                      128, D, [(0, 1024)], 'wpp', wst, defer=pre4)
        pre4.append(lambda: P.dma('sp', lambda e: e.dma_start(out=g2[:], in_=ln2g.partition_broadcast(128)), writes=['l2g']))
        pre4.append(lambda: P.dma('sp', lambda e: e.dma_start(out=b2[:], in_=ln2b.partition_broadcast(128)), writes=['l2b']))

        with contextlib.ExitStack() as es:
            XO = QB0 * 512
            x1T = SB(es, 'x1T', [128, 8, S - XO], BF16)
            cw = SB(es, 'cw', [128, NFC * 3], F32); cb_ = SB(es, 'cbv', [128, NFC], F32)
            wgs = Rot('wgs', [SB(es, 'wgs%d' % i, [128, 8, 128], F32) for i in range(2)])
            wus = Rot('wus', [SB(es, 'wus%d' % i, [128, 8, 128], F32) for i in range(2)])
            wgb = Rot('wgb', [SB(es, 'wgb%d' % i, [128, 8, 128], BF16) for i in range(2)])
            wub = Rot('wub', [SB(es, 'wub%d' % i, [128, 8, 128], BF16) for i in range(2)])
            gext = Rot('gext', [SB(es, 'gext%d' % i, [128, 514], F32) for i in range(2)])
            cv_ = Rot('cv', [SB(es, 'cv%d' % i, [128, 512], F32) for i in range(2)])
            ga = Rot('ga', [SB(es, 'ga%d' % i, [128, 512], F32) for i in range(2)])
            ast = Rot('ast', [SB(es, 'ast%d' % i, [128, 512], BF16) for i in range(3)])
            pg = Rot('pg', [PSB(es, 'pg%d' % i) for i in range(3)])
            pu = Rot('pu', [PSB(es, 'pu%d' % i) for i in range(3)])
            for kc in range(8):
                P.dma('sp', lambda e, kc=kc: e.dma_start(out=x1T[:, kc, :], in_=X1T[kc, :, XO:S]), reads=['X1T'], writes=['x1T'])
            hvs = SB(es, 'hvs', [128, 1], F32)
            P.dma('sp', lambda e: e.dma_start(out=hvs[:], in_=hvin[:, :]), writes=['hvs'])
            P.dma('sp', lambda e: e.dma_start(out=cw[:], in_=convw[:, :]), writes=['cw'])
            P.dma('sp', lambda e: e.dma_start(out=cb_[:], in_=convb[:, :]), writes=['cbv'])
            for j in range(NFC):
                wg_s, wgsn = wgs.next(); wu_s, wusn = wus.next(); wg_b, wgbn = wgb.next(); wu_b, wubn = wub.next()
                P.dma('sp', lambda e, wg_s=wg_s, j=j: e.dma_start(
                    out=wg_s[:], in_=w_ffn[:, j * 128:(j + 1) * 128].rearrange("(kc p) n -> p kc n", p=128)), writes=[wgsn])
                P.dma('sp', lambda e, wu_s=wu_s, j=j: e.dma_start(
                    out=wu_s[:], in_=w_ffn[:, DFF + j * 128:DFF + (j + 1) * 128].rearrange("(kc p) n -> p kc n", p=128)), writes=[wusn])
                P.op('pool', lambda e, wg_b=wg_b, wg_s=wg_s: e.tensor_copy(wg_b[:], wg_s[:]), reads=[wgsn, wgbn], writes=[wgbn])
                P.op('pool', lambda e, wu_b=wu_b, wu_s=wu_s: e.tensor_copy(wu_b[:], wu_s[:]), reads=[wusn, wubn], writes=[wubn])
                if j >= 1:
                    for _ in range(2):
                        if pre4:
                            pre4.pop(0)()
                prev = None
                ph_, phn = pg.next()
                for kc in range(8):
                    mm(ph_[:, 0:2], wg_b[:, kc, :], x1T[:, kc, OWN0 - 2 - XO:OWN0 - XO], kc == 0, kc == 7, [wgbn, 'x1T'], [phn])
                for tb in range(QB0 + 1, NQB):
                    t0 = tb * 512
                    pg_, pgn = pg.next(); pu_, pun = pu.next()
                    for kc in range(8):
                        mm(pg_[:, :], wg_b[:, kc, :], x1T[:, kc, t0 - XO:t0 - XO + 512], kc == 0, kc == 7, [wgbn, 'x1T'], [pgn])
                    for kc in range(8):
                        mm(pu_[:, :], wu_b[:, kc, :], x1T[:, kc, t0 - XO:t0 - XO + 512], kc == 0, kc == 7, [wubn, 'x1T'], [pun])
                    ge, gen = gext.next(); cvt, cvn = cv_.next(); gat, gan = ga.next(); at, atn = ast.next()
                    P.op('act', lambda e, ge=ge, pg_=pg_: e.copy(ge[:, 2:514], pg_[:, :]), reads=[pgn, gen], writes=[gen])
                    if prev is None:
                        P.op('dve', lambda e, ge=ge, ph_=ph_: e.tensor_scalar(ge[:, 0:2], ph_[:, 0:2], hvs[:, 0:1], None, ALU.mult),
                             reads=[gen, phn, 'hvs'], writes=[gen])
                    else:
                        P.op('pool', lambda e, ge=ge, pv=prev[0]: e.tensor_copy(ge[:, 0:2], pv[:, 512:514]), reads=[gen, prev[1]], writes=[gen])
                    prev = (ge, gen)
                    P.op('dve', lambda e, cvt=cvt, ge=ge, j=j: e.tensor_scalar(
                        cvt[:, :], ge[:, 2:514], cw[:, 3 * j + 2:3 * j + 3], cb_[:, j:j + 1], ALU.mult, ALU.add),
                        reads=[gen, 'cw', 'cbv', cvn], writes=[cvn])
                    P.op('dve', lambda e, cvt=cvt, ge=ge, j=j: e.scalar_tensor_tensor(
                        out=cvt[:, :], in0=ge[:, 1:513], scalar=cw[:, 3 * j + 1:3 * j + 2], in1=cvt[:, :], op0=ALU.mult, op1=ALU.add),
                        reads=[gen, 'cw', cvn], writes=[cvn])
                    P.op('dve', lambda e, cvt=cvt, ge=ge, j=j: e.scalar_tensor_tensor(
                        out=cvt[:, :], in0=ge[:, 0:512], scalar=cw[:, 3 * j:3 * j + 1], in1=cvt[:, :], op0=ALU.mult, op1=ALU.add),
                        reads=[gen, 'cw', cvn], writes=[cvn])
                    P.op('act', lambda e, gat=gat, cvt=cvt: e.activation(gat[:, :], cvt[:, :], AF.Gelu_apprx_tanh), reads=[cvn, gan], writes=[gan])
                    P.op('dve', lambda e, at=at, gat=gat, pu_=pu_: e.tensor_tensor(at[:, :], gat[:, :], pu_[:, :], ALU.mult),
                         reads=[gan, pun, atn], writes=[atn])
                    P.dma('sp', lambda e, at=at, j=j, t0=t0: e.dma_start(out=AT[j, :, t0:t0 + 512], in_=at[:, :]), reads=[atn], writes=['AT'])
            while pre4:
                pre4.pop(0)()
            P.barrier()
            P.emit()
        if DEBUG == 'p4a':
            return nc, 1

        with contextlib.ExitStack() as es:
            aTr = Rot('aT', [SB(es, 'aT%d' % i, [128, NFC, 512], BF16) for i in range(2)])
            xt = Rot('xt4', [SB(es, 'xt4_%d' % i, [128, D], F32) for i in range(2)])
            x2t = Rot('x2t', [SB(es, 'x2t%d' % i, [128, D], F32) for i in range(2)])
            x2b = Rot('x2b', [SB(es, 'x2b%d' % i, [128, D], BF16) for i in range(2)])
            x2s = Rot('x2s', [SB(es, 'x2s%d' % i, [128, 8, 128], BF16) for i in range(2)])
            sg = Rot('sg', [SB(es, 'sg%d' % i, [128, D], F32) for i in range(2)])
            pf = Rot('pf', [SB(es, 'pf%d' % i, [128, 256], F32) for i in range(2)])
            pbf = Rot('pbf', [SB(es, 'pbf%d' % i, [128, 256], BF16) for i in range(2)])
            pTs = Rot('pTs', [SB(es, 'pTs%d' % i, [128, 2, 128], BF16) for i in range(2)])
            lnt = (SB(es, 'l2st', [128, 12], F32), SB(es, 'l2mv', [128, 8], F32), SB(es, 'l2t1', [128, D], F32))
            pz = Rot('pz4', [PSB(es, 'pz4_%d' % i) for i in range(2)])
            pgt = Rot('pgt', [PSB(es, 'pgt%d' % i) for i in range(2)])
            ppp = Rot('ppp', [PSB(es, 'ppp%d' % i) for i in range(2)])
            pt4 = PSB(es, 'pt4', BF16)
            pt5 = PSB(es, 'pt5', BF16)
            for tb in range(QB0 + 1, NQB):
                t0 = tb * 512
                aT, aTn = aTr.next()
                for hfj in range(2):
                    P.dma('sp', lambda e, aT=aT, t0=t0, hfj=hfj: e.dma_start(
                        out=aT[:, hfj * 11:(hfj + 1) * 11, :], in_=AT[hfj * 11:(hfj + 1) * 11, :, t0:t0 + 512].rearrange("c p t -> p c t")),
                        reads=['AT'], writes=[aTn])
                for i in range(4):
                    tok0 = t0 + i * 128
                    xx, xxn = xt.next(); o2, o2n = x2t.next(); ob, obn = x2b.next(); xs_, xsn = x2s.next()
                    r_, rn = xx, xxn
                    sgt, sgn = sg.next(); pft, pfn = pf.next(); pbt, pbn_ = pbf.next(); pTt, pTn = pTs.next()
                    oo, oon = sgt, sgn
                    P.dma('sp', lambda e, xx=xx, tok0=tok0: e.dma_start(out=xx[:], in_=X1[tok0:tok0 + 128, :]), reads=['X1'], writes=[xxn])
                    P.dma('sp', lambda e, pft=pft, tok0=tok0: e.dma_start(out=pft[:], in_=pin[tok0 - OWN0:tok0 - OWN0 + 128, :]), writes=[pfn])
                    for half in range(2):
                        pz_, pzn = pz.next()
                        for j in range(NFC):
                            mm(pz_[:, :], aT[:, j, i * 128:(i + 1) * 128], wdn[:, j, half * 512:(half + 1) * 512], j == 0, j == NFC - 1,
                               [aTn, 'wdn'], [pzn])
                        P.op('dve', lambda e, r_=r_, xx=xx, pz_=pz_, half=half: e.scalar_tensor_tensor(
                            out=r_[:, half * 512:(half + 1) * 512], in0=xx[:, half * 512:(half + 1) * 512], scalar=ALPHA, in1=pz_[:, :],
                            op0=ALU.mult, op1=ALU.add), reads=[xxn, pzn], writes=[rn])
                    layernorm(es, 'l2', lnt, r_, rn, g2, b2, o2, o2n)
                    P.op('act', lambda e, ob=ob, o2=o2: e.copy(ob[:], o2[:]), reads=[o2n, obn], writes=[obn])
                    for kc in range(8):
                        tr(pt4[:, kc * 128:(kc + 1) * 128], ob[:, kc * 128:(kc + 1) * 128], [obn], ['pt4'])
                    P.op('act', lambda e, xs_=xs_: e.copy(xs_[:], pt4[:].rearrange("p (a b) -> p a b", b=128)), reads=['pt4', xsn], writes=[xsn])
                    P.op('pool', lambda e, pbt=pbt, pft=pft: e.tensor_copy(pbt[:], pft[:]), reads=[pfn, pbn_], writes=[pbn_])
                    for kc in range(2):
                        tr(pt5[:, kc * 128:(kc + 1) * 128], pbt[:, kc * 128:(kc + 1) * 128], [pbn_], ['pt5'])
                    P.op('act', lambda e, pTt=pTt: e.copy(pTt[:], pt5[:, 0:256].rearrange("p (a b) -> p a b", b=128)),
                         reads=['pt5', pTn], writes=[pTn])
                    for half in range(2):
                        pg_, pgn = pgt.next(); pp_, ppn = ppp.next()
                        for kc in range(8):
                            mm(pg_[:, :], xs_[:, kc, :], wpg[:, kc, half * 512:(half + 1) * 512], kc == 0, kc == 7, [xsn, 'wpg'], [pgn])
                        for kc in range(2):
                            mm(pp_[:, :], pTt[:, kc, :], wpp[:, kc, half * 512:(half + 1) * 512], kc == 0, kc == 1, [pTn, 'wpp'], [ppn])
                        P.op('act', lambda e, sgt=sgt, pg_=pg_, half=half: e.activation(
                            sgt[:, half * 512:(half + 1) * 512], pg_[:, :], AF.Sigmoid), reads=[pgn, sgn], writes=[sgn])
                        P.op('dve', lambda e, sgt=sgt, pp_=pp_, half=half: e.tensor_tensor(
                            sgt[:, half * 512:(half + 1) * 512], sgt[:, half * 512:(half + 1) * 512], pp_[:, :], ALU.mult),
                            reads=[ppn, sgn], writes=[sgn])
                    P.op('pool', lambda e, oo=oo, sgt=sgt, o2=o2: e.tensor_tensor(oo[:, :], sgt[:, :], o2[:, :], ALU.add),
                         reads=[sgn, o2n], writes=[oon])
                    P.dma('sp', lambda e, oo=oo, tok0=tok0: e.dma_start(out=out[tok0 - OWN0:tok0 - OWN0 + 128, :], in_=oo[:]), reads=[oon], writes=['out'])
            P.barrier()
            P.emit()
        es4.close()
    return nc, None
```
